# Optimizing a Trainium2 kernel written in Bass

```python
import jax, jax.numpy as jnp
from jax import lax
import numpy as np

D_MODEL = 2048
BATCH = 8
SEQ = 2048
DEPTH = 2

CTX_LEN = 256
GRID_W = 64
EPS = 1e-6

CHUNK = 128
GMLP_GROUPS = 8
W_A = D_MODEL // 2
GMLP_GW = W_A // GMLP_GROUPS

W_B = D_MODEL // 2
F_GROUPS = 4
F_GW = W_B // F_GROUPS

HEAD_DIM = 128
NA_HEADS = 8
W_C = NA_HEADS * HEAD_DIM
NA_KH = 8
NA_KW = 16
NA_BAND = 2 * NA_KW

OFF_U = 0
OFF_VG = OFF_U + W_A
OFF_GA = OFF_VG + W_A
OFF_F = OFF_GA + W_A
OFF_GF = OFF_F + W_B
OFF_Q = OFF_GF + W_B
OFF_K = OFF_Q + W_C
OFF_V = OFF_K + W_C
OFF_GN = OFF_V + W_C
OFF_MERGE = OFF_GN + W_C
W_IN = OFF_MERGE + 3 * D_MODEL

kernel_name = "hybrid_gmlp_fnet_natten_prefix_block"


def rms_norm(x, g):
    xf = x.astype(jnp.float32)
    y = xf * lax.rsqrt(jnp.mean(xf * xf, axis=-1, keepdims=True) + EPS)
    return (y * g.astype(jnp.float32)).astype(x.dtype)


def layer_norm(x, g, b):
    xf = x.astype(jnp.float32)
    mu = jnp.mean(xf, axis=-1, keepdims=True)
    xc = xf - mu
    y = xc * lax.rsqrt(jnp.mean(xc * xc, axis=-1, keepdims=True) + EPS)
    return (y * g.astype(jnp.float32) + b.astype(jnp.float32)).astype(x.dtype)


def heads(t):
    return t.reshape(t.shape[:-1] + (NA_HEADS, HEAD_DIM))


def chunk_spatial_gating(u, v, ln_g, ln_b, ws, bs):
    B, N, _ = u.shape
    u = jax.nn.gelu(u)
    v = layer_norm(jax.nn.gelu(v), ln_g, ln_b)
    vg = v.reshape(B, N // CHUNK, CHUNK, GMLP_GROUPS, GMLP_GW)
    sv = jnp.einsum('gpq,bnqgc->bnpgc', ws, vg) + bs.T[:, :, None]
    return u * sv.reshape(B, N, W_A)


def fourier_mix(xf):
    B, N, _ = xf.shape
    xg = xf.reshape(B, N, F_GROUPS, F_GW).astype(jnp.float32)
    y = jnp.fft.fft2(xg, axes=(1, 3), norm="ortho").real
    return y.reshape(B, N, W_B).astype(xf.dtype)


def context_self_attention(qc, kc, vc):
    s = jnp.einsum('bqhd,bkhd->bhqk', qc, kc, preferred_element_type=jnp.float32) * (HEAD_DIM ** -0.5)
    p = jax.nn.softmax(s, axis=-1).astype(vc.dtype)
    o = jnp.einsum('bhqk,bkhd->bqhd', p, vc)
    return o.reshape(o.shape[:2] + (W_C,))


def neighbourhood_attention(q, k, v, kc, vc, rpb):
    B, S, H, Dh = q.shape
    rows = S // GRID_W
    kh = min(NA_KH, rows)
    r = jnp.arange(rows)
    row_start = jnp.clip(r - kh // 2, 0, rows - kh)
    row_idx = row_start[:, None] + jnp.arange(kh)[None, :]
    dr = row_idx - r[:, None]
    qg = q.reshape(B, rows, GRID_W, H, Dh)
    kg = k.reshape(B, rows, GRID_W, H, Dh)
    vg = v.reshape(B, rows, GRID_W, H, Dh)
    scale = HEAD_DIM ** -0.5
    outs = []
    for j in range(GRID_W // NA_KW):
        q0 = j * NA_KW
        band0 = min(max(q0 - NA_KW // 2, 0), GRID_W - NA_BAND)
        cols_q = q0 + np.arange(NA_KW)
        col_start = np.clip(cols_q - NA_KW // 2, 0, GRID_W - NA_KW)
        cols_k = band0 + np.arange(NA_BAND)
        col_mask = jnp.asarray((cols_k[None, :] >= col_start[:, None])
                               & (cols_k[None, :] < col_start[:, None] + NA_KW))
        dc = jnp.asarray(cols_k[None, :] - cols_q[:, None])
        qb = qg[:, :, q0:q0 + NA_KW]
        kb = kg[:, :, band0:band0 + NA_BAND][:, row_idx]
        vb = vg[:, :, band0:band0 + NA_BAND][:, row_idx]
        s_loc = jnp.einsum('brqhd,brkmhd->bhrqkm', qb, kb,
                           preferred_element_type=jnp.float32) * scale
        bias = rpb[:, dr[:, None, :, None] + (NA_KH - 1),
                   dc[None, :, None, :] + (NA_KW - 1)]
        s_loc = jnp.where(col_mask[None, None, None, :, None, :],
                          s_loc + bias.astype(jnp.float32)[None], -jnp.inf)
        s_loc = s_loc.reshape(B, H, rows, NA_KW, kh * NA_BAND)
        s_ctx = jnp.einsum('brqhd,bkhd->bhrqk', qb, kc,
                           preferred_element_type=jnp.float32) * scale
        p = jax.nn.softmax(jnp.concatenate([s_loc, s_ctx], axis=-1), axis=-1).astype(v.dtype)
        p_loc = p[..., :kh * NA_BAND].reshape(B, H, rows, NA_KW, kh, NA_BAND)
        p_ctx = p[..., kh * NA_BAND:]
        o = (jnp.einsum('bhrqkm,brkmhd->brqhd', p_loc, vb)
             + jnp.einsum('bhrqk,bkhd->brqhd', p_ctx, vc))
        outs.append(o)
    o = jnp.concatenate(outs, axis=2)
    return o.reshape(B, S, W_C)


def merge_branches(z, attn, ln_g, ln_b, ws, bs, w_pa, w_pf, w_pn, w_out):
    a = chunk_spatial_gating(z[..., OFF_U:OFF_VG], z[..., OFF_VG:OFF_GA], ln_g, ln_b, ws, bs) \
        * jax.nn.silu(z[..., OFF_GA:OFF_F])
    f = fourier_mix(z[..., OFF_F:OFF_GF]) * jax.nn.silu(z[..., OFF_GF:OFF_Q])
    n = attn * jax.nn.silu(z[..., OFF_GN:OFF_MERGE])
    g = jax.nn.sigmoid(z[..., OFF_MERGE:])
    g_a, g_f, g_n = g[..., :D_MODEL], g[..., D_MODEL:2 * D_MODEL], g[..., 2 * D_MODEL:]
    y = g_a * (a @ w_pa) + g_f * (f @ w_pf) + g_n * (n @ w_pn)
    return y @ w_out


def setup_inputs(seed: int = 0) -> dict:
    key = jax.random.key(seed)
    ks = jax.random.split(key, 20)
    L, D = DEPTH, D_MODEL

    def nrm(k, shape, s):
        return jax.random.normal(k, shape, jnp.float32) * s

    return {
        "x": nrm(ks[0], (BATCH, SEQ, D), 1.0),
        "c": nrm(ks[1], (BATCH, D), 1.0),
        "ctx": nrm(ks[2], (BATCH, CTX_LEN, D), 1.0),
        "c_ctx": nrm(ks[3], (D,), 1.0),
        "norm_g": 1.0 + nrm(ks[4], (L, D), 0.02),
        "w_ada": nrm(ks[5], (L, D, 3 * D), 0.5 * D ** -0.5),
        "b_ada": nrm(ks[6], (L, 3 * D), 0.01),
        "w_in": nrm(ks[7], (L, D, W_IN), D ** -0.5),
        "gmlp_ln_g": 1.0 + nrm(ks[8], (L, W_A), 0.02),
        "gmlp_ln_b": nrm(ks[9], (L, W_A), 0.02),
        "gmlp_ws": nrm(ks[10], (L, GMLP_GROUPS, CHUNK, CHUNK), CHUNK ** -0.5),
        "gmlp_bs": 1.0 + nrm(ks[11], (L, GMLP_GROUPS, CHUNK), 0.02),
        "q_norm_g": 1.0 + nrm(ks[12], (L, HEAD_DIM), 0.02),
        "k_norm_g": 1.0 + nrm(ks[13], (L, HEAD_DIM), 0.02),
        "rpb": nrm(ks[14], (L, NA_HEADS, 2 * NA_KH - 1, 2 * NA_KW - 1), 0.1),
        "w_pa": nrm(ks[15], (L, W_A, D), W_A ** -0.5),
        "w_pf": nrm(ks[16], (L, W_B, D), W_B ** -0.5),
        "w_pn": nrm(ks[17], (L, W_C, D), W_C ** -0.5),
        "w_out": nrm(ks[18], (L, D, D), D ** -0.5),
    }


def reference(x, c, ctx, c_ctx, norm_g, w_ada, b_ada, w_in, gmlp_ln_g, gmlp_ln_b, gmlp_ws,
              gmlp_bs, q_norm_g, k_norm_g, rpb, w_pa, w_pf, w_pn, w_out):
    D = D_MODEL
    silu_c = jax.nn.silu(c)
    silu_cc = jax.nn.silu(c_ctx)
    for l in range(DEPTH):
        last = l == DEPTH - 1
        mod = silu_c @ w_ada[l] + b_ada[l]
        shift, scale, gate = mod[:, :D], mod[:, D:2 * D], mod[:, 2 * D:]
        mod_c = silu_cc @ w_ada[l] + b_ada[l]
        shift_c, scale_c, gate_c = mod_c[:D], mod_c[D:2 * D], mod_c[2 * D:]

        h = rms_norm(x, norm_g[l]) * (1.0 + scale[:, None, :]) + shift[:, None, :]
        hc = rms_norm(ctx, norm_g[l]) * (1.0 + scale_c) + shift_c

        if last:
            zkv = hc @ w_in[l][:, OFF_K:OFF_GN]
            kc = rms_norm(heads(zkv[..., :W_C]), k_norm_g[l])
            vc = heads(zkv[..., W_C:])
        else:
            zc = hc @ w_in[l]
            qc = rms_norm(heads(zc[..., OFF_Q:OFF_K]), q_norm_g[l])
            kc = rms_norm(heads(zc[..., OFF_K:OFF_V]), k_norm_g[l])
            vc = heads(zc[..., OFF_V:OFF_GN])
            attn_c = context_self_attention(qc, kc, vc)
            out_c = merge_branches(zc, attn_c, gmlp_ln_g[l], gmlp_ln_b[l], gmlp_ws[l], gmlp_bs[l],
                                   w_pa[l], w_pf[l], w_pn[l], w_out[l])

        z = h @ w_in[l]
        q = rms_norm(heads(z[..., OFF_Q:OFF_K]), q_norm_g[l])
        k = rms_norm(heads(z[..., OFF_K:OFF_V]), k_norm_g[l])
        v = heads(z[..., OFF_V:OFF_GN])
        attn = neighbourhood_attention(q, k, v, kc, vc, rpb[l])
        out = merge_branches(z, attn, gmlp_ln_g[l], gmlp_ln_b[l], gmlp_ws[l], gmlp_bs[l],
                             w_pa[l], w_pf[l], w_pn[l], w_out[l])
        x = x + gate[:, None, :] * out
        if not last:
            ctx = ctx + gate_c * out_c
    return x
```

```python
import contextlib
import numpy as np
import ml_dtypes
import concourse.bass as bass
import concourse.mybir as mybir
from concourse.bass_utils import run_bass_kernel_spmd

F32 = mybir.dt.float32
BF16 = mybir.dt.bfloat16
AF = mybir.ActivationFunctionType
ALU = mybir.AluOpType

D = 2048
SEQ = 2048
CTX = 256
NT = SEQ + CTX
DEPTH = 2
W_A = 1024
OFF_U, OFF_VG, OFF_GA, OFF_F, OFF_GF, OFF_Q, OFF_K, OFF_V, OFF_GN, OFF_MERGE = [1024 * i for i in range(10)]
W_IN = OFF_MERGE + 3 * D
EPS = 1e-6
NEG = -30000.0
NCORES = 8
import os
STOP = os.environ.get('K_STOP', '')

ENGS = ["pe", "act", "dve", "pool", "sp"]


class Res:
    _n = 0

    def __init__(self, name=""):
        Res._n += 1
        self.id = Res._n
        self.name = name
        self.writers = {}
        self.readers = {}
        self.war = {}
        self.box = None
        self.lock = False


class SemBox:
    _n = 0

    def __init__(self):
        SemBox._n += 1
        self.id = SemBox._n
        self.total = 0
        self.sem = None


class Prog:
    def __init__(self, nc):
        self.nc = nc
        self.ops = []
        self.boxes = []
        self.free_boxes = []
        self.barrier_toks = []
        self.latest = {}
        self.live = []

    def _collect(self, reads, writes, pw):
        deps = list(self.barrier_toks)
        for r in reads:
            deps.extend(r.writers.values())
        for w in writes:
            if w.lock:
                for t in list(w.writers.values()) + list(w.readers.values()) + list(w.war.values()):
                    deps.append(('lop', t[1]) if t[0] == 'op' else t)
                continue
            deps.extend(w.writers.values())
            deps.extend(w.readers.values())
            deps.extend(w.war.values())
        for w in pw:
            if w.readers:
                w.war = dict(w.readers)
                w.readers = {}
                w.writers = {}
            deps.extend(w.war.values())
        return deps

    def _commit(self, tok, key, reads, writes, pw):
        for r in reads:
            r.readers[key] = tok
        for w in writes:
            w.writers = {key: tok}
            w.readers = {}
            w.war = {}
        for w in pw:
            w.writers[key] = tok

    def op(self, eng, fn, reads=(), writes=(), pw=()):
        deps = self._collect(reads, writes, pw)
        idx = len(self.ops)
        self.ops.append(dict(eng=eng, fn=fn, deps=deps, dma=None))
        self._commit(('op', idx), eng, reads, writes, pw)
        self.latest[eng] = idx
        return idx

    def dma(self, eng, fn, slot, reads=(), writes=(), pw=()):
        deps = self._collect(reads, writes, pw)
        if slot.box is None:
            if self.free_boxes:
                slot.box = self.free_boxes.pop()
            else:
                slot.box = SemBox()
                self.boxes.append(slot.box)
            self.live.append(slot)
        box = slot.box
        if box.total > 0:
            deps.append(('dma', box, box.total))
        box.total += 16
        tok = ('dma', box, box.total)
        self.ops.append(dict(eng=eng, fn=fn, deps=deps, dma=(box, box.total)))
        self._commit(tok, ('dma', box.id), reads, writes, pw)
        return tok

    def barrier(self):
        toks = [('op', i) for i in self.latest.values()]
        toks += [('dma', b, b.total) for b in self.boxes if b.total > 0]
        self.barrier_toks = toks
        for s in self.live:
            self.free_boxes.append(s.box)
            s.box = None
        self.live = []

    def all_tokens(self):
        toks = [('op', i) for i in self.latest.values()]
        toks += [('dma', b, b.total) for b in self.boxes if b.total > 0]
        return toks

    def emit(self, final_waits=()):
        nc = self.nc
        ops = self.ops
        signaling = set()
        for o in ops:
            for d in o['deps']:
                if d[0] in ('op', 'lop'):
                    if ops[d[1]]['eng'] == 'pe' and o['eng'] == 'pe':
                        continue
                    if d[0] == 'lop' and ops[d[1]]['eng'] == o['eng']:
                        continue
                    signaling.add(d[1])
        for d in final_waits:
            if d[0] == 'op':
                signaling.add(d[1])
        cnt = {e: 0 for e in ENGS}
        seq = {}
        for i, o in enumerate(ops):
            if o['dma'] is None and i in signaling:
                cnt[o['eng']] += 1
                seq[i] = cnt[o['eng']]
        stack = contextlib.ExitStack()
        esem = {e: stack.enter_context(nc.semaphore("s_" + e)) for e in ENGS}
        for s in self.boxes:
            s.sem = stack.enter_context(nc.semaphore("d%d" % s.id))
        per_eng = {e: [] for e in ENGS}
        waited = {e: {} for e in ENGS}
        nwaits = 0
        for i, o in enumerate(ops):
            e = o['eng']
            ws = {}
            for d in o['deps']:
                if d[0] in ('op', 'lop'):
                    de = ops[d[1]]['eng']
                    if de == 'pe' and e == 'pe':
                        continue
                    if d[0] == 'lop' and de == e:
                        continue
                    key = ('e', de)
                    val = seq[d[1]]
                    sem = esem[de]
                else:
                    key = ('d', d[1].id)
                    val = d[2]
                    sem = d[1].sem
                if waited[e].get(key, 0) >= val:
                    continue
                if key not in ws or ws[key][1] < val:
                    ws[key] = (sem, val)
            for key, (sem, val) in ws.items():
                waited[e][key] = val
            nwaits += len(ws)
            sig = None
            if o['dma'] is not None:
                sig = (o['dma'][0].sem, 16)
            elif i in signaling:
                sig = (esem[e], 1)
            per_eng[e].append((list(ws.values()), o['fn'], sig))
        fw = []
        for d in final_waits:
            if d[0] == 'op':
                fw.append((esem[ops[d[1]]['eng']], seq[d[1]]))
            else:
                fw.append((d[1].sem, d[2]))
        self.stats = dict(n_ops=len(ops), n_waits=nwaits, per_eng={e: len(per_eng[e]) for e in ENGS},
                          n_sems=len(self.boxes) + 5)

        def run(engine, lst, extra=()):
            for waits, fn, sig in lst:
                for sem, val in waits:
                    engine.wait_ge(sem, val)
                ins = fn(engine)
                if sig is not None:
                    ins.then_inc(sig[0], sig[1])
            for sem, val in extra:
                engine.wait_ge(sem, val)

        with nc.Block() as block:
            @block.tensor
            def _(eng):
                run(eng, per_eng['pe'])

            @block.scalar
            def _(eng):
                run(eng, per_eng['act'])

            @block.vector
            def _(eng):
                run(eng, per_eng['dve'])

            @block.gpsimd
            def _(eng):
                run(eng, per_eng['pool'])

            @block.sync
            def _(eng):
                run(eng, per_eng['sp'], fw)
        stack.close()


def _u_list(t):
    if t <= 1:
        return [0, 1, 2, 3]
    if t >= 14:
        return [12, 13, 14, 15]
    return [t - 2, t - 1, t, t + 1, t + 2]


def _tile_base(t):
    return {0: 5, 1: 9, 14: 13, 15: 17}.get(t, 0)


def _bias_index():
    tiles = [(8, u) for u in _u_list(8)]
    for t in (0, 1, 14, 15):
        tiles += [(t, u) for u in _u_list(t)]
    idx = np.zeros((21, 128, 128), np.int64)
    k = np.arange(128)
    q = np.arange(128)
    for ti, (t, u) in enumerate(tiles):
        krow = (2 * u + k // 64)[:, None]
        kcol = (k % 64)[:, None]
        qrow = (2 * t + q // 64)[None, :]
        qcol = (q % 64)[None, :]
        rs = np.clip(qrow - 4, 0, 24)
        cs = np.clip(qcol - 8, 0, 48)
        valid = (krow >= rs) & (krow < rs + 8) & (kcol >= cs) & (kcol < cs + 16)
        dr = krow - qrow
        dc = kcol - qcol
        ii = (dr + 7) * 31 + (dc + 15)
        idx[ti] = np.where(valid, ii, 465)
    return idx


def _dft_consts():
    bf = ml_dtypes.bfloat16
    c = np.arange(256, dtype=np.float64)
    th = 2 * np.pi * np.outer(c, c) / 256.0
    ccsc = np.concatenate([np.cos(th), np.sin(th)], axis=1) / 16.0
    ccsc = ccsc.reshape(2, 128, 512).transpose(1, 0, 2)
    out = {"ccsc": np.ascontiguousarray(ccsc).astype(bf)}
    for N, tag in ((SEQ, "L"), (CTX, "C")):
        n = np.arange(N, dtype=np.float64)
        th = 2 * np.pi * (np.outer(n, n) % N) / N
        out["cn" + tag] = (np.cos(th) / np.sqrt(N)).astype(bf)
        out["sn" + tag] = (-np.sin(th) / np.sqrt(N)).astype(bf)
    out["ident"] = np.eye(128, dtype=np.float32).astype(bf)
    return out


def build_program(layers=(0, 1), debug=False):
    nc = bass.Bass("TRN2", target_bir_lowering=False)
    P = Prog(nc)
    L = DEPTH

    def din(name, shape, dt=F32):
        return nc.dram_tensor(name, list(shape), dt, kind="ExternalInput").ap()

    def dscr(name, shape, dt=BF16):
        if debug:
            return nc.dram_tensor(name, list(shape), dt, kind="ExternalOutput").ap()
        return nc.dram_tensor(name, list(shape), dt).ap()

    x_in = din("x", [SEQ, D])
    ctx_in = din("ctx", [CTX, D])
    c2_in = din("c2", [128, 16, 2])
    normg_in = din("norm_g", [L, 128, 16])
    bada_in = din("b_ada", [L, 128, 48])
    wada_in = din("w_ada", [L, D, 3 * D])
    win_in = din("w_in", [L, D, W_IN])
    lng_in = din("ln_g", [L, 128, W_A])
    lnb_in = din("ln_b", [L, 128, W_A])
    wst_in = din("wsT", [L, 128, 8, 128])
    bs_in = din("bs", [L, 128, 8, 128])
    qkg_in = din("qkg", [128, L, 2])
    rpbt_in = din("rpbt", [L, 8, 128, 21, 128])
    wpa_in = din("w_pa", [L, W_A, D])
    wpf_in = din("w_pf", [L, W_A, D])
    wpn_in = din("w_pn", [L, W_A, D])
    wout_in = din("w_out", [L, D, D])
    ccsc_in = din("ccsc", [128, 2, 512], BF16)
    cnL_in = din("cnL", [SEQ, SEQ], BF16)
    snL_in = din("snL", [SEQ, SEQ], BF16)
    cnC_in = din("cnC", [CTX, CTX], BF16)
    snC_in = din("snC", [CTX, CTX], BF16)
    ident_in = din("ident", [128, 128], BF16)
    out_d = nc.dram_tensor("out", [SEQ, D], F32, kind="ExternalOutput").ap()

    zT = dscr("zT", [W_IN, NT])
    zVG = dscr("zVG", [NT, W_A])
    zV = dscr("zV", [NT, W_A])
    aT_d = dscr("aT", [W_A, NT])
    fT_d = dscr("fT", [W_A, NT])
    nT_d = dscr("nT", [W_A, NT])
    yT_d = dscr("yT", [D, NT])
    x1_d = dscr("x1", [SEQ, D], F32)
    ctx1_d = dscr("ctx1", [CTX, D], F32)
    gate_d = dscr("gate", [L, 2, D], F32)

    R = {k: Res(k) for k in ["zT", "zVG", "zV", "aT", "fT", "nT", "yT", "x1", "ctx1", "gate", "out", "const"]}
    RZ = {k: Res("zT_" + k) for k in ["U", "GA", "F", "GF", "Q", "K", "GN", "M"]}

    gstack = contextlib.ExitStack()

    def SB(stack, name, shape, dt):
        return stack.enter_context(nc.sbuf_tensor("sb_" + name, list(shape), dt))

    def PS(stack, name, shape, dt=F32):
        return stack.enter_context(nc.psum_tensor("ps_" + name, list(shape), dt))

    uid = [0]

    def nm(s):
        uid[0] += 1
        return "%s_%d" % (s, uid[0])

    ident = SB(gstack, "ident", [128, 128], BF16)
    r_ident = Res("ident")
    P.dma('sp', lambda e: e.dma_start(out=ident[:], in_=ident_in), r_ident, writes=[r_ident])
    ones_bf = SB(gstack, "ones_bf", [128, 128], BF16)
    r_ones = Res("ones")
    P.op('dve', lambda e: e.memset(ones_bf[:], 1.0), writes=[r_ones])
    qkg = SB(gstack, "qkg", [128, L, 2], F32)
    r_qkg = Res("qkg")
    P.dma('sp', lambda e: e.dma_start(out=qkg[:], in_=qkg_in), r_qkg, writes=[r_qkg])
    qkgs = SB(gstack, "qkgs", [128, L, 2], F32)
    r_qkgs = Res("qkgs")
    P.op('dve', lambda e: e.tensor_copy(out=qkgs[:], in_=qkg[:]), reads=[r_qkg], writes=[r_qkgs])
    for l_ in range(L):
        P.op('dve', lambda e, l_=l_: e.tensor_scalar(out=qkgs[:, l_, 0:1], in0=qkg[:, l_, 0:1], scalar1=float(128 ** -0.5),
                                                  scalar2=None, op0=ALU.mult), reads=[r_qkg, r_qkgs], writes=[r_qkgs])
    c2 = SB(gstack, "c2", [128, 16, 2], F32)
    r_c2 = Res("c2")
    P.dma('sp', lambda e: e.dma_start(out=c2[:], in_=c2_in), r_c2, writes=[r_c2])
    sc2 = SB(gstack, "sc2", [128, 16, 2], BF16)
    r_sc2 = Res("sc2")
    P.op('act', lambda e: e.activation(out=sc2[:], in_=c2[:], func=AF.Silu), reads=[r_c2], writes=[r_sc2])
    eps_t = SB(gstack, "eps_t", [128, 1], F32)
    r_eps = Res("eps")
    P.op('dve', lambda e: e.memset(eps_t[:], EPS), writes=[r_eps])

    TB = [(0, 512), (512, 512), (1024, 512), (1536, 512), (2048, 256)]

    modv_l = [SB(gstack, "modv%d" % i, [128, 48, 2], F32) for i in range(L)]
    r_modv_l = [Res("modv%d" % i) for i in range(L)]
    s1v_l = [SB(gstack, "s1v%d" % i, [128, 16, 2], F32) for i in range(L)]
    r_s1v_l = [Res("s1v%d" % i) for i in range(L)]
    bada_l = [SB(gstack, "bada%d" % i, [128, 48], F32) for i in range(L)]
    normg_l = [SB(gstack, "normg%d" % i, [128, 16], F32) for i in range(L)]
    r_bn_l = [Res("bn%d" % i) for i in range(L)]
    for i_ in range(L):
        r_t1_, r_t2_ = Res("bnl1"), Res("bnl2")
        P.dma('sp', lambda e, i_=i_: e.dma_start(out=bada_l[i_][:], in_=bada_in[i_]), r_t1_, writes=[r_bn_l[i_]])
        P.dma('sp', lambda e, i_=i_: e.dma_start(out=normg_l[i_][:], in_=normg_in[i_]), r_t2_, writes=[r_bn_l[i_]])

    def skew(n, phases):
        k = len(phases)
        for s in range(n + k - 1):
            for j in reversed(range(k)):
                i = s - j
                if 0 <= i < n:
                    phases[j](i)

    def mod_steps(lm, wa, r_wa, ps_mod, r_psmod):
        modv, r_modv, s1v, r_s1v = modv_l[lm], r_modv_l[lm], s1v_l[lm], r_s1v_l[lm]
        bada, normg, r_bn = bada_l[lm], normg_l[lm], r_bn_l[lm]
        wada_v = wada_in[lm].rearrange("(k p) c -> p k c", p=128)

        def ld_wa(i):
            b = i % 2
            P.dma('pool', lambda e: e.dma_start(out=wa[b][:], in_=wada_v[:, :, i * 512:(i + 1) * 512]), r_wa[b],
                  writes=[r_wa[b]])

        def step(i):
            if i == 0:
                ld_wa(0)
            if i + 1 < 12:
                ld_wa(i + 1)
            b = i % 2
            for j in range(4):
                cj = i * 4 + j
                for k in range(16):
                    P.op('pe', lambda e, b=b, j=j, k=k, cj=cj: e.matmul(ps_mod[:, cj, :], lhsT=wa[b][:, k, j * 128:(j + 1) * 128],
                                                                    rhs=sc2[:, k, :], start=(k == 0), stop=(k == 15)),
                         reads=[r_wa[b], r_sc2], writes=[r_psmod])

        r_modg = Res("modg")

        def post_a(_):
            for j in range(2):
                P.op('dve', lambda e, j=j: e.tensor_tensor(out=modv[:, 0:32, j], in0=ps_mod[:, 0:32, j], in1=bada[:, 0:32], op=ALU.add),
                     reads=[r_psmod, r_bn], pw=[r_modv])
            for j in range(2):
                P.op('dve', lambda e, j=j: e.scalar_tensor_tensor(out=s1v[:, :, j], in0=modv[:, 16:32, j], scalar=1.0, in1=normg[:],
                                                                op0=ALU.add, op1=ALU.mult),
                     reads=[r_modv, r_bn], pw=[r_s1v])

        def post_b(_):
            for j in range(2):
                P.op('dve', lambda e, j=j: e.tensor_tensor(out=modv[:, 32:48, j], in0=ps_mod[:, 32:48, j], in1=bada[:, 32:48], op=ALU.add),
                     reads=[r_psmod, r_bn], pw=[r_modg])
            r_gslot = Res("gslot")

            def gate_bounce(e, j):
                with nc.allow_non_contiguous_dma(reason="tiny gate vector bounce"):
                    return e.dma_start(out=gate_d[lm, j].rearrange("(k p) -> p k", p=128), in_=modv[:, 32:48, j])
            for j in range(2):
                P.dma('sp', lambda e, j=j: gate_bounce(e, j), r_gslot, reads=[r_modg], writes=[R["gate"]])
        return ([(lambda i=i: step(i)) for i in range(8)] + [lambda: post_a(0)] +
                [(lambda i=i: step(i)) for i in range(8, 12)] + [lambda: post_b(0)])

    def skew_steps_g(n, phases):
        k = len(phases)
        steps = []
        for s_ in range(n + k - 1):
            def stp(s_=s_):
                for j in reversed(range(k)):
                    i = s_ - j
                    if 0 <= i < n:
                        phases[j](i)
            steps.append(stp)
        return steps

    def gmlp_steps(st, l, TBL):
        lng = SB(st, nm("lng"), [128, W_A], F32)
        lnb = SB(st, nm("lnb"), [128, W_A], F32)
        bsb = SB(st, nm("bsb"), [128, 8, 128], F32)
        wsf = SB(st, nm("wsf"), [128, 8, 128], F32)
        wsb = SB(st, nm("wsb"), [128, 8, 128], BF16)
        r_ln = Res("ln")
        r_wsf = Res("wsf")
        r_wsb = Res("wsb")
        r_l1, r_l2, r_l3 = Res("l1"), Res("l2"), Res("l3")
        P.dma('sp', lambda e: e.dma_start(out=lng[:], in_=lng_in[l]), r_l1, pw=[r_ln])
        P.dma('sp', lambda e: e.dma_start(out=lnb[:], in_=lnb_in[l]), r_l2, pw=[r_ln])
        P.dma('sp', lambda e: e.dma_start(out=bsb[:], in_=bs_in[l]), r_l3, pw=[r_ln])
        P.dma('sp', lambda e: e.dma_start(out=wsf[:], in_=wst_in[l]), r_wsf, writes=[r_wsf])
        P.op('dve', lambda e: e.tensor_copy(out=wsb[:], in_=wsf[:]), reads=[r_wsf], writes=[r_wsb])
        vpre = [SB(st, nm("vpre"), [128, W_A], BF16) for _ in range(2)]
        r_vpre = [Res("vpre0"), Res("vpre1")]
        vf = SB(st, nm("vf"), [128, W_A], F32)
        r_vf = Res("vf")
        vf2 = SB(st, nm("vf2"), [128, W_A], F32)
        r_vf2 = Res("vf2")
        vb = [SB(st, nm("vb"), [128, W_A], BF16) for _ in range(2)]
        r_vb = [Res("vb0"), Res("vb1")]
        stt = [SB(st, nm("stt"), [128, 2, 6], F32) for _ in range(2)]
        mv = [SB(st, nm("mv"), [128, 4], F32) for _ in range(2)]
        r_mv = [Res("mv0"), Res("mv1")]
        UT = [SB(st, nm("UT"), [128, 8, 128], BF16) for _ in range(2)]
        GAT = [SB(st, nm("GAT"), [128, 8, 128], BF16) for _ in range(2)]
        r_UT = [Res("UT0"), Res("UT1")]
        r_GAT = [Res("GAT0"), Res("GAT1")]
        t1 = [SB(st, nm("t1"), [128, 8, 128], F32) for _ in range(2)]
        r_t1 = [Res("t10"), Res("t11")]
        t2 = SB(st, nm("t2"), [128, 8, 128], F32)
        r_t2 = Res("t2")
        ast = [SB(st, nm("ast"), [128, 8, 128], BF16) for _ in range(2)]
        r_ast = [Res("ast0"), Res("ast1")]
        psv = [PS(st, nm("psv"), [128, 8, 128]) for _ in range(2)]
        r_psv = [Res("psv0"), Res("psv1")]
        zU_v = zT[OFF_U:OFF_U + W_A, :].rearrange("(g p) t -> p g t", p=128)
        zGA_v = zT[OFF_GA:OFF_GA + W_A, :].rearrange("(g p) t -> p g t", p=128)
        aT_v = aT_d.rearrange("(g p) t -> p g t", p=128)
        toks = []
        for (t0, tn) in TBL:
            for ci in range(tn // 128):
                toks.append(t0 + ci * 128)

        def g_load(c):
            cb = c % 2
            tok0 = toks[c]
            P.dma('sp', lambda e: e.dma_start(out=vpre[cb][:], in_=zVG[tok0:tok0 + 128, :]), r_vpre[cb],
                  reads=[R["zVG"]], writes=[r_vpre[cb]])

        def g_p1(c):
            cb = c % 2
            for hh in range(2):
                P.op('dve', lambda e, hh=hh: e.bn_stats(out=stt[cb][:, hh, :], in_=vpre[cb][:, hh * 512:(hh + 1) * 512]),
                     reads=[r_vpre[cb]], writes=[r_mv[cb]])
            P.op('dve', lambda e: e.bn_aggr(out=mv[cb][:, 0:2], in_=stt[cb][:].rearrange("p a b -> p (a b)")),
                 reads=[r_mv[cb]], writes=[r_mv[cb]])
            P.op('act', lambda e: e.activation(out=mv[cb][:, 2:3], in_=mv[cb][:, 1:2], func=AF.Sqrt, scale=1.0, bias=eps_t[:]),
                 reads=[r_mv[cb], r_eps], writes=[r_mv[cb]])
            P.op('dve', lambda e: e.reciprocal(out=mv[cb][:, 3:4], in_=mv[cb][:, 2:3]), reads=[r_mv[cb]], writes=[r_mv[cb]])
            P.op('dve', lambda e: e.scalar_tensor_tensor(out=mv[cb][:, 2:3], in0=mv[cb][:, 0:1], scalar=-1.0, in1=mv[cb][:, 3:4],
                                                         op0=ALU.mult, op1=ALU.mult), reads=[r_mv[cb]], writes=[r_mv[cb]])
            P.op('act', lambda e: e.activation(out=vf[:], in_=vpre[cb][:], func=AF.Identity, scale=mv[cb][:, 3:4],
                                               bias=mv[cb][:, 2:3]), reads=[r_vpre[cb], r_mv[cb]], writes=[r_vf])

        def g_p2(c):
            cb = c % 2
            tok0 = toks[c]
            P.dma('sp', lambda e: e.dma_start(out=UT[cb][:], in_=zU_v[:, :, tok0:tok0 + 128]), r_UT[cb],
                  reads=[RZ["U"]], writes=[r_UT[cb]])
            P.dma('sp', lambda e: e.dma_start(out=GAT[cb][:], in_=zGA_v[:, :, tok0:tok0 + 128]), r_GAT[cb],
                  reads=[RZ["GA"]], writes=[r_GAT[cb]])
            P.op('pool', lambda e: e.tensor_tensor(out=vf2[:], in0=vf[:], in1=lng[:], op=ALU.mult), reads=[r_vf, r_ln],
                 writes=[r_vf2])
            P.op('pool', lambda e: e.tensor_tensor(out=vb[cb][:], in0=vf2[:], in1=lnb[:], op=ALU.add),
                 reads=[r_vf2, r_ln], writes=[r_vb[cb]])

        def g_p3(c):
            cb = c % 2
            for g in range(8):
                P.op('pe', lambda e, g=g: e.matmul(psv[cb][:, g, :], lhsT=vb[cb][:, g * 128:(g + 1) * 128], rhs=wsb[:, g, :],
                                                   start=True, stop=True),
                     reads=[r_vb[cb], r_wsb], writes=[r_psv[cb]])
            P.op('pool', lambda e: e.tensor_tensor(out=t1[cb][:], in0=UT[cb][:], in1=GAT[cb][:], op=ALU.mult),
                 reads=[r_UT[cb], r_GAT[cb]], writes=[r_t1[cb]])

        def g_p4(c):
            cb = c % 2
            tok0 = toks[c]
            P.op('dve', lambda e: e.tensor_tensor(out=t2[:], in0=psv[cb][:], in1=bsb[:], op=ALU.add),
                 reads=[r_psv[cb], r_ln], writes=[r_t2])
            P.op('dve', lambda e: e.tensor_tensor(out=ast[cb][:], in0=t1[cb][:], in1=t2[:], op=ALU.mult),
                 reads=[r_t1[cb], r_t2], writes=[r_ast[cb]])
            P.dma('sp', lambda e: e.dma_start(out=aT_v[:, :, tok0:tok0 + 128], in_=ast[cb][:]), r_ast[cb],
                  reads=[r_ast[cb]], pw=[R["aT"]])
        return skew_steps_g(len(toks), [g_load, g_p1, g_p2, g_p3, g_p4])

    def run_layer(l):
        last = (l == DEPTH - 1)
        xs_d, cs_d = (x_in, ctx_in) if l == 0 else (x1_d, ctx1_d)
        r_xs, r_cs = (R["const"], R["const"]) if l == 0 else (R["x1"], R["ctx1"])
        xo_d, r_xo = (out_d, R["out"]) if last else (x1_d, R["x1"])
        if len(layers) == 1:
            xo_d, r_xo = out_d, R["out"]
        modv, r_modv, s1v, r_s1v = modv_l[l], r_modv_l[l], s1v_l[l], r_s1v_l[l]

        hstack = contextlib.ExitStack()
        hT = SB(hstack, nm("hT"), [128, 16, NT], BF16)
        r_hT = Res("hT")

        P.barrier()
        st = contextlib.ExitStack()
        mtail = {}
        if l == layers[0]:
            wa = [SB(st, nm("wa"), [128, 16, 512], BF16) for _ in range(2)]
            r_wa = [Res("wa0"), Res("wa1")]
            ps_mod = PS(st, nm("ps_mod"), [128, 256, 2])
            r_psmod = Res("psmod")
            msteps0 = mod_steps(l, wa, r_wa, ps_mod, r_psmod)
            for f in msteps0[:9]:
                f()
            for i_, tt_ in enumerate((2, 6, 10, 14)):
                mtail[tt_] = msteps0[9 + i_]
        xt = [SB(st, nm("xt"), [128, D], F32) for _ in range(3)]
        r_xt = [Res("xt%d" % i) for i in range(3)]
        junk = SB(st, nm("junk"), [128, D], BF16)
        r_junk = Res("junk")
        xn = [SB(st, nm("xn"), [128, D], BF16) for _ in range(2)]
        r_xn = [Res("xn0"), Res("xn1")]
        ssq = [SB(st, nm("ssq"), [128, 4], F32) for _ in range(2)]
        r_ssq = [Res("ssq0"), Res("ssq1")]
        ps_tr = [PS(st, nm("ps_tr"), [128, 16, 128], BF16) for _ in range(2)]
        r_pstr = [Res("pstr0"), Res("pstr1")]

        def a_load(tt):
            b = tt % 3
            if tt < 16:
                src, rs = xs_d[tt * 128:(tt + 1) * 128, :], r_xs
            else:
                src, rs = cs_d[(tt - 16) * 128:(tt - 15) * 128, :], r_cs
            P.dma('sp', lambda e: e.dma_start(out=xt[b][:], in_=src), r_xt[b], reads=[rs], writes=[r_xt[b]])

        def a_p1(tt):
            b3, b = tt % 3, tt % 2
            P.op('act', lambda e: e.activation(out=junk[:], in_=xt[b3][:], func=AF.Square, accum_out=ssq[b][:, 0:1]),
                 reads=[r_xt[b3]], writes=[r_junk, r_ssq[b]])
            P.op('act', lambda e: e.activation(out=ssq[b][:, 1:2], in_=ssq[b][:, 0:1], func=AF.Sqrt, scale=1.0 / D, bias=eps_t[:]),
                 reads=[r_ssq[b], r_eps], writes=[r_ssq[b]])
            P.op('dve', lambda e: e.reciprocal(out=ssq[b][:, 2:3], in_=ssq[b][:, 1:2]), reads=[r_ssq[b]], writes=[r_ssq[b]])

        def a_p1b(tt):
            b3, b = tt % 3, tt % 2
            P.op('dve', lambda e: e.tensor_scalar(out=xn[b][:], in0=xt[b3][:], scalar1=ssq[b][:, 2:3], scalar2=None, op0=ALU.mult),
                 reads=[r_xt[b3], r_ssq[b]], writes=[r_xn[b]])

        def a_p2(tt):
            b = tt % 2
            if tt in mtail:
                mtail[tt]()
            for k in range(16):
                P.op('pe', lambda e, k=k: e.transpose(ps_tr[b][:, k, :], xn[b][:, k * 128:(k + 1) * 128], ident[:]),
                     reads=[r_xn[b], r_ident], writes=[r_pstr[b]])

        def a_p3(tt):
            b = tt % 2
            j = 0 if tt < 16 else 1
            for k in range(16):
                if k >= 8:
                    P.op('act', lambda e, k=k: e.activation(out=hT[:, k, tt * 128:(tt + 1) * 128], in_=ps_tr[b][:, k, :],
                                                            func=AF.Identity, scale=s1v[:, k, j:j + 1], bias=modv[:, k, j:j + 1]),
                         reads=[r_pstr[b], r_s1v, r_modv], pw=[r_hT])
                else:
                    P.op('dve', lambda e, k=k: e.tensor_scalar(out=hT[:, k, tt * 128:(tt + 1) * 128], in0=ps_tr[b][:, k, :],
                                                              scalar1=s1v[:, k, j:j + 1], scalar2=modv[:, k, j:j + 1],
                                                              op0=ALU.mult, op1=ALU.add),
                         reads=[r_pstr[b], r_s1v, r_modv], pw=[r_hT])
        skew(18, [a_load, a_p1, a_p1b, a_p2, a_p3])
        if l == layers[0]:
            msteps0[13]()
        P.barrier()
        st.close()
        if STOP == 'A':
            hstack.close()
            return

        st = contextlib.ExitStack()
        wb = [SB(st, nm("wb"), [128, 16, 512], BF16) for _ in range(2)]
        r_wb = [Res("wb0"), Res("wb1")]
        NZ = 3
        zst = [SB(st, nm("zst"), [128, NT], BF16) for _ in range(NZ)]
        r_zst = [Res("zst%d" % i) for i in range(NZ)]
        ztm = [SB(st, nm("ztm"), [128, 9, 512], BF16) for _ in range(2)]
        r_ztm = [Res("ztm0"), Res("ztm1")]
        pz = [PS(st, nm("pz"), [128, 512]) for _ in range(4)]
        r_pz = [Res("pz%d" % i) for i in range(4)]
        NTL = SEQ if last else NT
        TBL = [tb for tb in TB if tb[0] < NTL]
        g_steps = gmlp_steps(st, l, TBL)
        win_v = win_in[l].rearrange("(k p) c -> p k c", p=128)
        groups = ["U", "VG", "GA", "F", "GF", "Q", "K", "V", "GN"] + ["M"] * 6
        gfunc = {"U": AF.Gelu_apprx_tanh, "VG": AF.Gelu_apprx_tanh, "GA": AF.Silu, "GF": AF.Silu, "GN": AF.Silu,
                 "M": AF.Sigmoid}

        def ld_w(i):
            b = i % 2
            P.dma('pool', lambda e: e.dma_start(out=wb[b][:], in_=win_v[:, :, i * 512:(i + 1) * 512]), r_wb[b],
                  writes=[r_wb[b]])
        ld_w(0)
        cnts = dict(p=0, z=0, tm=0, cpy=0)

        def s2_tile(i):
            if i + 1 < 30:
                ld_w(i + 1)
            b = i % 2
            g = groups[i // 2]
            c0 = i * 512
            if g in ("VG", "V"):
                dst = zVG if g == "VG" else zV
                r_dst = R["zVG"] if g == "VG" else R["zV"]
                cc0 = c0 - (OFF_VG if g == "VG" else OFF_V)
                dst_v = dst[:, cc0:cc0 + 512].rearrange("(t p) c -> p t c", p=128)
                for hf in range(2):
                    tb_ = cnts['tm'] % 2
                    cnts['tm'] += 1
                    for t9 in range(9):
                        tt = hf * 9 + t9
                        pb = cnts['p'] % 4
                        cnts['p'] += 1
                        for k in range(16):
                            P.op('pe', lambda e, pb=pb, k=k, tt=tt: e.matmul(pz[pb][:], lhsT=hT[:, k, tt * 128:(tt + 1) * 128],
                                                                          rhs=wb[b][:, k, :], start=(k == 0), stop=(k == 15)),
                                 reads=[r_hT, r_wb[b]], writes=[r_pz[pb]])
                        if g == "VG":
                            P.op('act', lambda e, pb=pb, t9=t9, tb_=tb_: e.activation(out=ztm[tb_][:, t9, :], in_=pz[pb][:],
                                                                                    func=AF.Gelu_apprx_tanh),
                                 reads=[r_pz[pb]], pw=[r_ztm[tb_]])
                        else:
                            P.op('dve', lambda e, pb=pb, t9=t9, tb_=tb_: e.tensor_copy(out=ztm[tb_][:, t9, :], in_=pz[pb][:]),
                                 reads=[r_pz[pb]], pw=[r_ztm[tb_]])
                    P.dma('sp', lambda e, tb_=tb_, hf=hf: e.dma_start(out=dst_v[:, hf * 9:(hf + 1) * 9, :], in_=ztm[tb_][:]),
                          r_ztm[tb_], reads=[r_ztm[tb_]], pw=[r_dst])
                return
            for sbk in range(4):
                zb = cnts['z'] % NZ
                cnts['z'] += 1
                ntok = NT
                for (t0, tn) in TB:
                    if last and t0 >= SEQ and g != "K":
                        ntok = SEQ
                        continue
                    pb = cnts['p'] % 4
                    cnts['p'] += 1
                    for k in range(16):
                        P.op('pe', lambda e, pb=pb, k=k, t0=t0, tn=tn, sbk=sbk: e.matmul(
                            pz[pb][:, 0:tn], lhsT=wb[b][:, k, sbk * 128:(sbk + 1) * 128], rhs=hT[:, k, t0:t0 + tn],
                            start=(k == 0), stop=(k == 15)), reads=[r_hT, r_wb[b]], writes=[r_pz[pb]])
                    if g in gfunc:
                        P.op('act', lambda e, pb=pb, zb=zb, t0=t0, tn=tn, f=gfunc[g]: e.activation(
                            out=zst[zb][:, t0:t0 + tn], in_=pz[pb][:, 0:tn], func=f), reads=[r_pz[pb]], pw=[r_zst[zb]])
                    else:
                        cnts['cpy'] += 1
                        if cnts['cpy'] % 3 == 0:
                            P.op('act', lambda e, pb=pb, zb=zb, t0=t0, tn=tn: e.activation(
                                out=zst[zb][:, t0:t0 + tn], in_=pz[pb][:, 0:tn], func=AF.Copy), reads=[r_pz[pb]], pw=[r_zst[zb]])
                        else:
                            P.op('dve', lambda e, pb=pb, zb=zb, t0=t0, tn=tn: e.tensor_copy(
                                out=zst[zb][:, t0:t0 + tn], in_=pz[pb][:, 0:tn]), reads=[r_pz[pb]], pw=[r_zst[zb]])
                r0 = c0 + sbk * 128
                P.dma('sp', lambda e, zb=zb, r0=r0, ntok=ntok: e.dma_start(out=zT[r0:r0 + 128, 0:ntok], in_=zst[zb][:, 0:ntok]),
                      r_zst[zb], reads=[r_zst[zb]], pw=[RZ[g]])

        for i in range(30):
            s2_tile(i)
            if i >= 6 and g_steps:
                g_steps.pop(0)()
        while g_steps:
            g_steps.pop(0)()
        P.barrier()
        st.close()
        hstack.close()
        if STOP == 'S2':
            return

        st = contextlib.ExitStack()
        ccsc = SB(st, nm("ccsc"), [128, 2, 512], BF16)
        r_ccsc = Res("ccsc")
        P.dma('sp', lambda e: e.dma_start(out=ccsc[:], in_=ccsc_in), r_ccsc, writes=[r_ccsc])
        XCS = SB(st, nm("XCS"), [128, 16, 4, 512], BF16)
        r_XCS = Res("XCS")
        zFT = [SB(st, nm("zFT"), [128, 2, SEQ], BF16) for _ in range(2)]
        r_zFT = [Res("zFT0"), Res("zFT1")]
        cnb = [SB(st, nm("cnb"), [128, 16, 512], BF16) for _ in range(2)]
        snb = [SB(st, nm("snb"), [128, 16, 512], BF16) for _ in range(2)]
        r_cnb = [Res("cnb0"), Res("cnb1")]
        r_snb = [Res("snb0"), Res("snb1")]
        GFT = [SB(st, nm("GFT"), [128, 8, 512], BF16) for _ in range(2)]
        r_GFT = [Res("GFT0"), Res("GFT1")]
        fst = [SB(st, nm("fst"), [128, 8, 512], BF16) for _ in range(2)]
        r_fst = [Res("fst0"), Res("fst1")]
        pf1 = [PS(st, nm("pf1"), [128, 512]) for _ in range(4)]
        r_pf1 = [Res("pf1%d" % i) for i in range(4)]
        pf2 = [PS(st, nm("pf2"), [128, 512]) for _ in range(2)]
        r_pf2 = [Res("pf20"), Res("pf21")]
        zF_v = zT[OFF_F:OFF_F + 1024, :].rearrange("(G j p) t -> G p j t", j=2, p=128)
        zGF_v = zT[OFF_GF:OFF_GF + 1024, :].rearrange("(c p) t -> p c t", p=128)
        fT_v = fT_d.rearrange("(c p) t -> p c t", p=128)
        fcnt = dict(z=0, p1=0, c=0, p2=0)
        jobs = [(SEQ, 0, cnL_in, snL_in)] + ([] if last else [(CTX, SEQ, cnC_in, snC_in)])
        s1_units = [(N, tok0, G) for (N, tok0, _, _) in jobs for G in range(4)]
        s2_units = [(N, tok0, cn_d, sn_d, nb) for (N, tok0, cn_d, sn_d) in jobs for nb in range(max(1, N // 512))]

        def f_ld1(i):
            if i >= len(s1_units):
                return
            N, tok0, G = s1_units[i]
            zb = i % 2
            P.dma('sp', lambda e: e.dma_start(out=zFT[zb][:, :, 0:N], in_=zF_v[G][:, :, tok0:tok0 + N]), r_zFT[zb],
                  reads=[RZ["F"]], writes=[r_zFT[zb]])

        def f_ld2(i):
            if i >= len(s2_units):
                return
            N, tok0, cn_d, sn_d, nb = s2_units[i]
            T = N // 128
            nbw = min(512, N)
            cb = i % 2
            cn_v = cn_d.rearrange("(t p) c -> p t c", p=128)
            sn_v = sn_d.rearrange("(t p) c -> p t c", p=128)
            P.dma('sp', lambda e: e.dma_start(out=cnb[cb][:, 0:T, 0:nbw], in_=cn_v[:, :, nb * nbw:(nb + 1) * nbw]),
                  r_cnb[cb], reads=[R["const"]], writes=[r_cnb[cb]])
            P.dma('sp', lambda e: e.dma_start(out=snb[cb][:, 0:T, 0:nbw], in_=sn_v[:, :, nb * nbw:(nb + 1) * nbw]),
                  r_snb[cb], reads=[R["const"]], writes=[r_snb[cb]])
            c0 = tok0 + nb * nbw
            P.dma('sp', lambda e: e.dma_start(out=GFT[cb][:, :, 0:nbw], in_=zGF_v[:, :, c0:c0 + nbw]), r_GFT[cb],
                  reads=[RZ["GF"]], writes=[r_GFT[cb]])

        f_ld1(0)
        f_ld2(0)
        i2c = [0]

        def f_job(ji, N, tok0, cn_d, sn_d):
                T = N // 128
                nbw = min(512, N)
                for G in range(4):
                    i1 = ji * 4 + G
                    zb = i1 % 2
                    f_ld1(i1 + 1)
                    for t in range(T):
                        pb = fcnt['p1'] % 4
                        fcnt['p1'] += 1
                        for j in range(2):
                            P.op('pe', lambda e, pb=pb, zb=zb, j=j, t=t: e.matmul(pf1[pb][:], lhsT=zFT[zb][:, j, t * 128:(t + 1) * 128],
                                                                               rhs=ccsc[:, j, :], start=(j == 0), stop=(j == 1)),
                                 reads=[r_zFT[zb], r_ccsc], writes=[r_pf1[pb]])
                        if t % 2 == 0:
                            P.op('dve', lambda e, pb=pb, t=t, G=G: e.tensor_copy(out=XCS[:, t, G, :], in_=pf1[pb][:]),
                                 reads=[r_pf1[pb]], pw=[r_XCS])
                        else:
                            P.op('act', lambda e, pb=pb, t=t, G=G: e.activation(out=XCS[:, t, G, :], in_=pf1[pb][:], func=AF.Copy),
                                 reads=[r_pf1[pb]], pw=[r_XCS])
                for nb in range(N // nbw):
                    cb = i2c[0] % 2
                    i2c[0] += 1
                    f_ld2(i2c[0])
                    c0 = tok0 + nb * nbw
                    for G in range(4):
                        for j in range(2):
                            pb = fcnt['p2'] % 2
                            fcnt['p2'] += 1
                            for t in range(T):
                                P.op('pe', lambda e, pb=pb, t=t, G=G, j=j, cb=cb: e.matmul(
                                    pf2[pb][:, 0:nbw], lhsT=XCS[:, t, G, j * 128:(j + 1) * 128], rhs=cnb[cb][:, t, 0:nbw],
                                    start=(t == 0), stop=False), reads=[r_XCS, r_cnb[cb]], writes=[r_pf2[pb]])
                            for t in range(T):
                                P.op('pe', lambda e, pb=pb, t=t, G=G, j=j, cb=cb: e.matmul(
                                    pf2[pb][:, 0:nbw], lhsT=XCS[:, t, G, 256 + j * 128:256 + (j + 1) * 128], rhs=snb[cb][:, t, 0:nbw],
                                    start=False, stop=(t == T - 1)), reads=[r_XCS, r_snb[cb]], writes=[r_pf2[pb]])
                            cch = G * 2 + j
                            P.op('dve', lambda e, pb=pb, cb=cb, cch=cch: e.tensor_tensor(out=fst[cb][:, cch, 0:nbw], in0=pf2[pb][:, 0:nbw],
                                                                                         in1=GFT[cb][:, cch, 0:nbw], op=ALU.mult),
                                 reads=[r_pf2[pb], r_GFT[cb]], pw=[r_fst[cb]])
                    P.dma('sp', lambda e, cb=cb, c0=c0: e.dma_start(out=fT_v[:, :, c0:c0 + nbw], in_=fst[cb][:, :, 0:nbw]), r_fst[cb],
                          reads=[r_fst[cb]], writes=[R["fT"]])
        for ji, (N, tok0, cn_d, sn_d) in enumerate(jobs):
            f_job(ji, N, tok0, cn_d, sn_d)
        P.barrier()
        st.close()

        if STOP == 'S3b':
            return
        wstack = contextlib.ExitStack()
        wp = [SB(wstack, nm("wp"), [128, 8, D], BF16) for _ in range(3)]
        r_wp = [[Res("wp%d%d" % (i, hf)) for hf in range(2)] for i in range(3)]
        for i, wsrc in enumerate((wpa_in, wpf_in, wpn_in)):
            for hf in range(2):
                P.dma('pool', lambda e, i=i, wsrc=wsrc, hf=hf: e.dma_start(
                    out=wp[i][:, :, hf * 1024:(hf + 1) * 1024],
                    in_=wsrc[l].rearrange("(k p) c -> p k c", p=128)[:, :, hf * 1024:(hf + 1) * 1024]), r_wp[i][hf],
                    writes=[r_wp[i][hf]])
        st = contextlib.ExitStack()
        QT1 = SB(st, nm("QT"), [128, NT], BF16)
        KT1 = SB(st, nm("KT"), [128, NT], BF16)
        QT = [QT1, QT1]
        KT = [KT1, KT1]
        r_QT1, r_KT1 = Res("QT"), Res("KT")
        r_QT = [r_QT1, r_QT1]
        r_KT = [r_KT1, r_KT1]
        QN = [SB(st, nm("QN"), [128, NT], BF16) for _ in range(2)]
        KN = [SB(st, nm("KN"), [128, NT], BF16) for _ in range(2)]
        r_QN = [Res("QN0"), Res("QN1")]
        r_KN = [Res("KN0"), Res("KN1")]
        V1 = [SB(st, nm("V1"), [128, 18, 132], BF16) for _ in range(2)]
        r_V1 = [Res("V10"), Res("V11")]
        for b in range(2):
            P.op('dve', lambda e, b=b: e.memset(V1[b][:], 1.0), writes=[r_V1[b]])
        GNT = [SB(st, nm("GNT"), [128, NT], BF16) for _ in range(2)]
        r_GNT = [Res("GNT0"), Res("GNT1")]
        BTf = SB(st, nm("BTf"), [128, 21, 128], F32)
        r_BTf = Res("BTf")
        BT = [SB(st, nm("BT"), [128, 21, 128], BF16) for _ in range(2)]
        r_BT = [Res("BT0"), Res("BT1")]
        nst = [SB(st, nm("nst"), [128, NT], BF16) for _ in range(2)]
        r_nst = [Res("nst0"), Res("nst1")]
        sq = [SB(st, nm("sq"), [128, 512], BF16) for _ in range(2)]
        r_sq = [Res("sq0"), Res("sq1")]
        lnt = [SB(st, nm("lnt"), [128, 512], F32) for _ in range(2)]
        r_lnt = [Res("lnt0"), Res("lnt1")]
        rt = [SB(st, nm("rt"), [128, 512], F32) for _ in range(2)]
        r_rt = [Res("rt0"), Res("rt1")]
        Eb = [SB(st, nm("Eb"), [128, 896], BF16) for _ in range(2)]
        r_Eb = [Res("Eb0"), Res("Eb1")]
        on = [SB(st, nm("on"), [128, 128], BF16) for _ in range(2)]
        r_on = [Res("on0"), Res("on1")]
        rd = [SB(st, nm("rd"), [128, 2], F32) for _ in range(2)]
        r_rd = [Res("rd0"), Res("rd1")]
        pss = [PS(st, nm("pss"), [128, 1024]) for _ in range(2)]
        r_pss = [Res("pss0"), Res("pss1")]
        bk = [PS(st, nm("bk"), [128, 512]) for _ in range(2)]
        r_bk = [Res("bk0"), Res("bk1")]
        for r_ in r_bk:
            r_.lock = True
        pso_v = [bk[i][:, 0:129] for i in range(2)]
        pst_v = [bk[i][:, 256:320].bitcast(BF16) for i in range(2)]
        psn = [PS(st, nm("psn"), [128, 512]) for _ in range(2)]
        r_psn = [Res("psn0"), Res("psn1")]
        nq_tiles = 16 if last else 18

        def c_loads(h):
            if h >= 8:
                return
            hb = h % 2
            P.dma('sp', lambda e: e.dma_start(out=QT[hb][:, 0:NTL], in_=zT[OFF_Q + h * 128:OFF_Q + (h + 1) * 128, 0:NTL]),
                  r_QT[hb], reads=[RZ["Q"]], writes=[r_QT[hb]])
            P.dma('sp', lambda e: e.dma_start(out=KT[hb][:], in_=zT[OFF_K + h * 128:OFF_K + (h + 1) * 128, :]),
                  r_KT[hb], reads=[RZ["K"]], writes=[r_KT[hb]])
            P.dma('sp', lambda e: e.dma_start(out=V1[hb][:, :, 0:128],
                                              in_=zV[:, h * 128:(h + 1) * 128].rearrange("(t p) d -> p t d", p=128)),
                  r_V1[hb], reads=[R["zV"]], writes=[r_V1[hb]])
            P.dma('sp', lambda e: e.dma_start(out=GNT[hb][:, 0:NTL], in_=zT[OFF_GN + h * 128:OFF_GN + (h + 1) * 128, 0:NTL]),
                  r_GNT[hb], reads=[RZ["GN"]], writes=[r_GNT[hb]])
            P.dma('sp', lambda e: e.dma_start(out=BTf[:], in_=rpbt_in[l, h]), r_BTf, reads=[R["const"]], writes=[r_BTf])
            P.op('pool', lambda e: e.tensor_copy(out=BT[hb][:], in_=BTf[:]), reads=[r_BTf], writes=[r_BT[hb]])

        def norm_phases(h):
            hb = h % 2
            blocks = []
            for (src, r_src, dstn, r_dstn, gi, ntk) in ((QT[hb], r_QT[hb], QN[hb], r_QN[hb], 0, NTL),
                                                        (KT[hb], r_KT[hb], KN[hb], r_KN[hb], 1, NT)):
                for (t0, tn) in TB:
                    if t0 < ntk:
                        blocks.append((src, r_src, dstn, r_dstn, gi, t0, tn))

            def n0(i):
                src, r_src, dstn, r_dstn, gi, t0, tn = blocks[i]
                nb_ = i % 2
                P.op('pool', lambda e: e.tensor_tensor(out=sq[nb_][:, 0:tn], in0=src[:, t0:t0 + tn], in1=src[:, t0:t0 + tn], op=ALU.mult),
                     reads=[r_src], writes=[r_sq[nb_]])

            def n1(i):
                src, r_src, dstn, r_dstn, gi, t0, tn = blocks[i]
                nb_ = i % 2
                P.op('pe', lambda e: e.matmul(psn[nb_][:, 0:tn], lhsT=ones_bf[:], rhs=sq[nb_][:, 0:tn], start=True, stop=True),
                     reads=[r_sq[nb_], r_ones], writes=[r_psn[nb_]])

            def n2(i):
                src, r_src, dstn, r_dstn, gi, t0, tn = blocks[i]
                nb_ = i % 2
                P.op('act', lambda e: e.activation(out=lnt[nb_][:, 0:tn], in_=psn[nb_][:, 0:tn], func=AF.Ln, scale=1.0 / 128.0,
                                                   bias=eps_t[:]), reads=[r_psn[nb_], r_eps], writes=[r_lnt[nb_]])
                P.op('act', lambda e: e.activation(out=rt[nb_][:, 0:tn], in_=lnt[nb_][:, 0:tn], func=AF.Exp, scale=-0.5),
                     reads=[r_lnt[nb_]], writes=[r_rt[nb_]])

            def n3(i):
                src, r_src, dstn, r_dstn, gi, t0, tn = blocks[i]
                nb_ = i % 2
                P.op('dve', lambda e: e.scalar_tensor_tensor(out=dstn[:, t0:t0 + tn], in0=src[:, t0:t0 + tn], scalar=qkgs[:, l, gi:gi + 1],
                                                             in1=rt[nb_][:, 0:tn], op0=ALU.mult, op1=ALU.mult),
                     reads=[r_src, r_rt[nb_], r_qkgs], pw=[r_dstn])
            return len(blocks), [n0, n1, n2, n3]

        def skew_steps(n, phases):
            k = len(phases)
            steps = []
            for s in range(n + k - 1):
                def stp(s=s):
                    for j in reversed(range(k)):
                        i = s - j
                        if 0 <= i < n:
                            phases[j](i)
                steps.append(stp)
            return steps

        units = [(h, t) for h in range(8) for t in range(nq_tiles)]

        def geom(u):
            h, t = units[u]
            if t < 16:
                us = _u_list(t)
                base = _tile_base(t)
            else:
                us, base = [], 0
            chunks_ = [uu * 128 for uu in us] + [SEQ, SEQ + 128]
            vt = list(us) + [16, 17]
            return h, t, h % 2, us, base, chunks_, vt, len(us), len(chunks_), t * 128

        pending = {'norm': []}

        def c_p0(u):
            h, t, hb, us, base, chunks_, vt, nloc, nch, q0 = geom(u)
            ub = u % 2
            if t == 6:
                c_loads(h + 1)
                if h + 1 < 8:
                    nbk, ph = norm_phases(h + 1)
                    pending['norm'] = skew_steps(nbk, ph)
            for i, k0 in enumerate(chunks_):
                loc = i < nloc
                P.op('pe', lambda e, i=i, k0=k0, loc=loc: e.matmul(pss[ub][:, i * 128:(i + 1) * 128], lhsT=KN[hb][:, k0:k0 + 128],
                                                                  rhs=QN[hb][:, q0:q0 + 128], start=True, stop=(not loc)),
                     reads=[r_KN[hb], r_QN[hb]], writes=[r_pss[ub]])
                if loc:
                    P.op('pe', lambda e, i=i: e.matmul(pss[ub][:, i * 128:(i + 1) * 128], lhsT=ident[:], rhs=BT[hb][:, base + i, :],
                                                       start=False, stop=True),
                         reads=[r_ident, r_BT[hb]], writes=[r_pss[ub]])

        def c_p1(u):
            h, t, hb, us, base, chunks_, vt, nloc, nch, q0 = geom(u)
            ub = u % 2
            P.op('act', lambda e: e.activation(out=Eb[ub][:, 0:nch * 128], in_=pss[ub][:, 0:nch * 128], func=AF.Exp),
                 reads=[r_pss[ub]], writes=[r_Eb[ub]])

        def c_p2(u):
            h, t, hb, us, base, chunks_, vt, nloc, nch, q0 = geom(u)
            ub = u % 2
            for i in range(nch):
                P.op('pe', lambda e, i=i: e.matmul(pso_v[ub], lhsT=Eb[ub][:, i * 128:(i + 1) * 128], rhs=V1[hb][:, vt[i], 0:129],
                                                   start=(i == 0), stop=(i == nch - 1)),
                     reads=[r_Eb[ub], r_V1[hb]], writes=[r_bk[ub]])

        def c_p3(u):
            ub = u % 2
            P.op('dve', lambda e: e.reciprocal(out=rd[ub][:, 0:1], in_=bk[ub][:, 128:129]), writes=[r_bk[ub], r_rd[ub]])
            P.op('dve', lambda e: e.tensor_scalar(out=on[ub][:], in0=bk[ub][:, 0:128], scalar1=rd[ub][:, 0:1], scalar2=None, op0=ALU.mult),
                 reads=[r_rd[ub]], writes=[r_bk[ub], r_on[ub]])

        def c_p4(u):
            ub = u % 2
            P.op('pe', lambda e: e.transpose(pst_v[ub], on[ub][:], ident[:]), reads=[r_on[ub], r_ident], writes=[r_bk[ub]])

        def c_pn(u):
            for _ in range(2):
                if pending['norm']:
                    pending['norm'].pop(0)()

        def c_p5(u):
            h, t, hb, us, base, chunks_, vt, nloc, nch, q0 = geom(u)
            ub = u % 2
            P.op('dve', lambda e: e.tensor_tensor(out=nst[hb][:, q0:q0 + 128], in0=pst_v[ub], in1=GNT[hb][:, q0:q0 + 128],
                                                  op=ALU.mult), reads=[r_GNT[hb]], writes=[r_bk[ub]], pw=[r_nst[hb]])
            if t == nq_tiles - 1:
                P.dma('sp', lambda e: e.dma_start(out=nT_d[h * 128:(h + 1) * 128, 0:NTL], in_=nst[hb][:, 0:NTL]), r_nst[hb],
                      reads=[r_nst[hb]], writes=[R["nT"]])

        c_loads(0)
        nbk0, ph0 = norm_phases(0)
        skew(nbk0, ph0)
        skew(len(units), [c_p0, c_p1, c_p2, c_p3, c_p4, c_p5, c_pn])
        P.barrier()
        st.close()
        if STOP == 'S3c':
            wstack.close()
            return

        st = contextlib.ExitStack()
        br = [[SB(st, nm("br"), [128, 8, 512], BF16) for _ in range(3)] for _ in range(2)]
        r_br = [[Res("br") for _ in range(3)] for _ in range(2)]
        gt = [SB(st, nm("gt"), [128, 3, 512], BF16) for _ in range(3)]
        r_gt = [Res("gt%d" % i) for i in range(3)]
        yy = [[SB(st, nm("yy"), [128, 512], F32) for _ in range(3)] for _ in range(2)]
        r_yy = [[Res("yy") for _ in range(3)] for _ in range(2)]
        yst = [SB(st, nm("yst"), [128, 16, 512], BF16) for _ in range(2)]
        r_yst = [Res("yst0"), Res("yst1")]
        pp = [PS(st, nm("pp"), [128, 512]) for _ in range(6)]
        r_pp = [Res("pp%d" % i) for i in range(6)]
        br_v = [d_.rearrange("(k p) t -> p k t", p=128) for d_ in (aT_d, fT_d, nT_d)]
        r_brd = [R["aT"], R["fT"], R["nT"]]
        zM_v = zT[OFF_MERGE:OFF_MERGE + 3 * D, :].rearrange("(j c p) t -> c p j t", j=3, c=16, p=128)
        yT_v = yT_d.rearrange("(c p) t -> p c t", p=128)
        units4 = [(bi, dc) for bi in range(len(TBL)) for dc in range(16)]

        def d_brload(bi):
            if bi >= len(TBL):
                return
            t0, tn = TBL[bi]
            bb = bi % 2
            for i in range(3):
                P.dma('sp', lambda e, i=i: e.dma_start(out=br[bb][i][:, :, 0:tn], in_=br_v[i][:, :, t0:t0 + tn]), r_br[bb][i],
                      reads=[r_brd[i]], writes=[r_br[bb][i]])
        d_brload(0)

        def d_load(u):
            bi, dc = units4[u]
            t0, tn = TBL[bi]
            gb = u % 3
            if dc == 2:
                d_brload(bi + 1)
            P.dma('sp', lambda e: e.dma_start(out=gt[gb][:, :, 0:tn], in_=zM_v[dc][:, :, t0:t0 + tn]), r_gt[gb],
                  reads=[RZ["M"]], writes=[r_gt[gb]])

        def d_p1(u):
            bi, dc = units4[u]
            t0, tn = TBL[bi]
            bb, pb = bi % 2, u % 2
            for i in range(3):
                pi = pb * 3 + i
                for k in range(8):
                    P.op('pe', lambda e, pi=pi, i=i, k=k: e.matmul(pp[pi][:, 0:tn], lhsT=wp[i][:, k, dc * 128:(dc + 1) * 128],
                                                                  rhs=br[bb][i][:, k, 0:tn], start=(k == 0), stop=(k == 7)),
                         reads=[r_wp[i][dc // 8], r_br[bb][i]], writes=[r_pp[pi]])

        def d_p2(u):
            bi, dc = units4[u]
            t0, tn = TBL[bi]
            bb, pb, gb = bi % 2, u % 2, u % 3
            for i in range(3):
                pi = pb * 3 + i
                P.op('dve', lambda e, pi=pi, i=i: e.tensor_tensor(out=yy[pb][i][:, 0:tn], in0=pp[pi][:, 0:tn], in1=gt[gb][:, i, 0:tn],
                                                                  op=ALU.mult),
                     reads=[r_pp[pi], r_gt[gb]], writes=[r_yy[pb][i]])
            P.op('pool', lambda e: e.tensor_tensor(out=yy[pb][0][:, 0:tn], in0=yy[pb][0][:, 0:tn], in1=yy[pb][1][:, 0:tn], op=ALU.add),
                 reads=[r_yy[pb][0], r_yy[pb][1]], writes=[r_yy[pb][0]])
            P.op('pool', lambda e: e.tensor_tensor(out=yst[bb][:, dc, 0:tn], in0=yy[pb][0][:, 0:tn], in1=yy[pb][2][:, 0:tn], op=ALU.add),
                 reads=[r_yy[pb][0], r_yy[pb][2]], pw=[r_yst[bb]])
            if dc == 15:
                P.dma('sp', lambda e: e.dma_start(out=yT_v[:, :, t0:t0 + tn], in_=yst[bb][:, :, 0:tn]), r_yst[bb], reads=[r_yst[bb]],
                      writes=[R["yT"]])
        skew(len(units4), [d_load, d_p1, d_p2])
        P.barrier()
        st.close()
        wstack.close()
        if STOP == 'S4a':
            return

        st = contextlib.ExitStack()
        wo = SB(st, nm("wo"), [128, 16, D], BF16)
        r_wo = [Res("wo%d" % i) for i in range(4)]
        for hf in range(4):
            P.dma('pool', lambda e, hf=hf: e.dma_start(out=wo[:, :, hf * 512:(hf + 1) * 512],
                                                      in_=wout_in[l].rearrange("(k p) c -> p k c", p=128)[:, :, hf * 512:(hf + 1) * 512]),
                  r_wo[hf], writes=[r_wo[hf]])
        gbc = SB(st, nm("gbc"), [128, 2, D], F32)
        r_gbc = Res("gbc")
        r_gb2 = Res("gb2")
        P.dma('sp', lambda e: e.dma_start(out=gbc[:, 0, :], in_=gate_d[l, 0].partition_broadcast(128)), r_gbc, reads=[R["gate"]],
              writes=[r_gbc])
        P.dma('sp', lambda e: e.dma_start(out=gbc[:, 1, :], in_=gate_d[l, 1].partition_broadcast(128)), r_gb2, reads=[R["gate"]],
              writes=[r_gbc])
        yt_ = [SB(st, nm("yt"), [128, 16, 512], BF16) for _ in range(2)]
        r_yt = [Res("yt0"), Res("yt1")]
        xr = [SB(st, nm("xr"), [128, D], F32) for _ in range(3)]
        r_xr = [Res("xr%d" % i) for i in range(3)]
        xo = [SB(st, nm("xo"), [128, D], F32) for _ in range(2)]
        r_xo_s = [Res("xo0"), Res("xo1")]
        tg = [SB(st, nm("tg"), [128, 512], F32) for _ in range(2)]
        r_tg = [Res("tg0"), Res("tg1")]
        po = [PS(st, nm("po"), [128, 512]) for _ in range(4)]
        r_po = [Res("po%d" % i) for i in range(4)]
        msteps = []
        if (not last) and (l + 1) in layers:
            wa = [SB(st, nm("wa"), [128, 16, 512], BF16) for _ in range(2)]
            r_wa = [Res("wa0"), Res("wa1")]
            ps_mod = PS(st, nm("ps_mod"), [128, 256, 2])
            r_psmod = Res("psmod")
            msteps = mod_steps(l + 1, wa, r_wa, ps_mod, r_psmod)
        tiles = []
        for bi, (t0, tn) in enumerate(TBL):
            for ti in range(tn // 128):
                tiles.append((bi, t0, tn, ti))
        units5 = [(tix, db) for tix in range(len(tiles)) for db in range(4)]

        def e_ytload(bi):
            if bi >= len(TBL):
                return
            t0, tn = TBL[bi]
            yb = bi % 2
            P.dma('sp', lambda e: e.dma_start(out=yt_[yb][:, :, 0:tn], in_=yT_v[:, :, t0:t0 + tn]), r_yt[yb],
                  reads=[R["yT"]], writes=[r_yt[yb]])
        e_ytload(0)

        def tile_io(tix):
            bi, t0, tn, ti = tiles[tix]
            tok0 = t0 + ti * 128
            if tok0 >= SEQ:
                return (cs_d[tok0 - SEQ:tok0 - SEQ + 128, :], r_cs, ctx1_d[tok0 - SEQ:tok0 - SEQ + 128, :], R["ctx1"], 1)
            return (xs_d[tok0:tok0 + 128, :], r_xs, xo_d[tok0:tok0 + 128, :], r_xo, 0)

        def e_load(u):
            tix, db = units5[u]
            bi, t0, tn, ti = tiles[tix]
            if ti == 0 and db == 2:
                e_ytload(bi + 1)
            if db != 0:
                return
            src, rs, dst, rdst, j = tile_io(tix)
            xb = tix % 3
            P.dma('sp', lambda e: e.dma_start(out=xr[xb][:], in_=src), r_xr[xb], reads=[rs], writes=[r_xr[xb]])

        def e_p1(u):
            tix, db = units5[u]
            bi, t0, tn, ti = tiles[tix]
            yb, pb = bi % 2, u % 4
            if msteps and u % 4 == 0:
                msteps.pop(0)()
            for k in range(16):
                P.op('pe', lambda e, k=k: e.matmul(po[pb][:], lhsT=yt_[yb][:, k, ti * 128:(ti + 1) * 128],
                                                   rhs=wo[:, k, db * 512:(db + 1) * 512], start=(k == 0), stop=(k == 15)),
                     reads=[r_yt[yb], r_wo[db]], writes=[r_po[pb]])

        def e_p2(u):
            tix, db = units5[u]
            src, rs, dst, rdst, j = tile_io(tix)
            pb, tb_, xb3, xb = u % 4, u % 2, tix % 3, tix % 2
            P.op('dve', lambda e: e.tensor_tensor(out=tg[tb_][:], in0=po[pb][:], in1=gbc[:, j, db * 512:(db + 1) * 512], op=ALU.mult),
                 reads=[r_po[pb], r_gbc], writes=[r_tg[tb_]])
            P.op('pool', lambda e: e.tensor_tensor(out=xo[xb][:, db * 512:(db + 1) * 512], in0=tg[tb_][:],
                                                   in1=xr[xb3][:, db * 512:(db + 1) * 512], op=ALU.add),
                 reads=[r_tg[tb_], r_xr[xb3]], pw=[r_xo_s[xb]])
            if db == 3:
                P.dma('sp', lambda e: e.dma_start(out=dst, in_=xo[xb][:]), r_xo_s[xb], reads=[r_xo_s[xb]], writes=[rdst])
        skew(len(units5), [e_load, e_p1, e_p2])
        while msteps:
            msteps.pop(0)()
        P.barrier()
        st.close()

    for l_i in layers:
        run_layer(l_i)
    P.emit(final_waits=P.all_tokens())
    gstack.close()
    return nc, P


_CACHE = {}


def _prep_inputs(inp):
    f32 = np.float32
    L = DEPTH
    consts = _dft_consts()
    idx = _bias_index()
    rpb = np.asarray(inp["rpb"], f32).reshape(L, 8, 15 * 31)
    tab = np.concatenate([rpb, np.full((L, 8, 1), NEG, f32)], axis=2)
    rpbt = tab[:, :, idx]
    rpbt = np.ascontiguousarray(rpbt.transpose(0, 1, 3, 2, 4))
    shared = {
        "norm_g": np.ascontiguousarray(np.asarray(inp["norm_g"], f32).reshape(L, 16, 128).transpose(0, 2, 1)),
        "b_ada": np.ascontiguousarray(np.asarray(inp["b_ada"], f32).reshape(L, 48, 128).transpose(0, 2, 1)),
        "w_ada": np.ascontiguousarray(np.asarray(inp["w_ada"], f32)),
        "w_in": np.ascontiguousarray(np.asarray(inp["w_in"], f32)),
        "ln_g": np.ascontiguousarray(np.broadcast_to(np.asarray(inp["gmlp_ln_g"], f32)[:, None, :], (L, 128, W_A))),
        "ln_b": np.ascontiguousarray(np.broadcast_to(np.asarray(inp["gmlp_ln_b"], f32)[:, None, :], (L, 128, W_A))),
        "wsT": np.ascontiguousarray(np.asarray(inp["gmlp_ws"], f32).transpose(0, 3, 1, 2)),
        "bs": np.ascontiguousarray(np.broadcast_to(np.asarray(inp["gmlp_bs"], f32)[:, None, :, :], (L, 128, 8, 128))),
        "qkg": np.ascontiguousarray(np.stack([np.asarray(inp["q_norm_g"], f32), np.asarray(inp["k_norm_g"], f32)], axis=-1)
                                    .transpose(1, 0, 2)),
        "rpbt": rpbt,
        "w_pa": np.ascontiguousarray(np.asarray(inp["w_pa"], f32)),
        "w_pf": np.ascontiguousarray(np.asarray(inp["w_pf"], f32)),
        "w_pn": np.ascontiguousarray(np.asarray(inp["w_pn"], f32)),
        "w_out": np.ascontiguousarray(np.asarray(inp["w_out"], f32)),
        "ccsc": consts["ccsc"], "cnL": consts["cnL"], "snL": consts["snL"], "cnC": consts["cnC"], "snC": consts["snC"],
        "ident": consts["ident"],
    }
    x = np.asarray(inp["x"], f32)
    ctx = np.asarray(inp["ctx"], f32)
    c = np.asarray(inp["c"], f32)
    c_ctx = np.asarray(inp["c_ctx"], f32)
    maps = []
    for b in range(NCORES):
        c2 = np.stack([c[b], c_ctx], axis=-1).reshape(16, 128, 2).transpose(1, 0, 2)
        m = dict(shared)
        m["x"] = np.ascontiguousarray(x[b])
        m["ctx"] = np.ascontiguousarray(ctx[b])
        m["c2"] = np.ascontiguousarray(c2)
        maps.append(m)
    return maps


def kernel(**inputs):
    if "nc" not in _CACHE:
        _CACHE["nc"] = build_program()[0]
    nc = _CACHE["nc"]
    maps = _prep_inputs(inputs)
    res = run_bass_kernel_spmd(nc, maps, core_ids=list(range(NCORES)))
    out = np.stack([np.asarray(r["out"], np.float32) for r in res.results], axis=0)
    return out
```

```python
import contextlib
import numpy as np
import ml_dtypes
import concourse.bass as bass
import concourse.mybir as mybir
from concourse.bass_utils import run_bass_kernel_spmd

F32 = mybir.dt.float32
BF16 = mybir.dt.bfloat16
AF = mybir.ActivationFunctionType
ALU = mybir.AluOpType

D = 2048
SEQ = 2048
CTX = 256
NT = SEQ + CTX
DEPTH = 2
W_A = 1024
OFF_U, OFF_VG, OFF_GA, OFF_F, OFF_GF, OFF_Q, OFF_K, OFF_V, OFF_GN, OFF_MERGE = [1024 * i for i in range(10)]
W_IN = OFF_MERGE + 3 * D
EPS = 1e-6
NEG = -30000.0
NCORES = 8
import os
STOP = os.environ.get('K_STOP', '')

ENGS = ["pe", "act", "dve", "pool", "sp"]


class Res:
    _n = 0

    def __init__(self, name=""):
        Res._n += 1
        self.id = Res._n
        self.name = name
        self.writers = {}
        self.readers = {}
        self.war = {}
        self.box = None
        self.lock = False


class SemBox:
    _n = 0

    def __init__(self):
        SemBox._n += 1
        self.id = SemBox._n
        self.total = 0
        self.sem = None


class Prog:
    def __init__(self, nc):
        self.nc = nc
        self.ops = []
        self.boxes = []
        self.free_boxes = []
        self.barrier_toks = []
        self.latest = {}
        self.live = []

    def _collect(self, reads, writes, pw):
        deps = list(self.barrier_toks)
        for r in reads:
            deps.extend(r.writers.values())
        for w in writes:
            if w.lock:
                for t in list(w.writers.values()) + list(w.readers.values()) + list(w.war.values()):
                    deps.append(('lop', t[1]) if t[0] == 'op' else t)
                continue
            deps.extend(w.writers.values())
            deps.extend(w.readers.values())
            deps.extend(w.war.values())
        for w in pw:
            if w.readers:
                w.war = dict(w.readers)
                w.readers = {}
                w.writers = {}
            deps.extend(w.war.values())
        return deps

    def _commit(self, tok, key, reads, writes, pw):
        for r in reads:
            r.readers[key] = tok
        for w in writes:
            w.writers = {key: tok}
            w.readers = {}
            w.war = {}
        for w in pw:
            w.writers[key] = tok

    def op(self, eng, fn, reads=(), writes=(), pw=()):
        deps = self._collect(reads, writes, pw)
        idx = len(self.ops)
        self.ops.append(dict(eng=eng, fn=fn, deps=deps, dma=None))
        self._commit(('op', idx), eng, reads, writes, pw)
        self.latest[eng] = idx
        return idx

    def dma(self, eng, fn, slot, reads=(), writes=(), pw=()):
        deps = self._collect(reads, writes, pw)
        if slot.box is None:
            if self.free_boxes:
                slot.box = self.free_boxes.pop()
            else:
                slot.box = SemBox()
                self.boxes.append(slot.box)
            self.live.append(slot)
        box = slot.box
        if box.total > 0:
            deps.append(('dma', box, box.total))
        box.total += 16
        tok = ('dma', box, box.total)
        self.ops.append(dict(eng=eng, fn=fn, deps=deps, dma=(box, box.total)))
        self._commit(tok, ('dma', box.id), reads, writes, pw)
        return tok

    def barrier(self):
        toks = [('op', i) for i in self.latest.values()]
        toks += [('dma', b, b.total) for b in self.boxes if b.total > 0]
        self.barrier_toks = toks
        for s in self.live:
            self.free_boxes.append(s.box)
            s.box = None
        self.live = []

    def all_tokens(self):
        toks = [('op', i) for i in self.latest.values()]
        toks += [('dma', b, b.total) for b in self.boxes if b.total > 0]
        return toks

    def emit(self, final_waits=()):
        nc = self.nc
        ops = self.ops
        signaling = set()
        for o in ops:
            for d in o['deps']:
                if d[0] in ('op', 'lop'):
                    if ops[d[1]]['eng'] == 'pe' and o['eng'] == 'pe':
                        continue
                    if d[0] == 'lop' and ops[d[1]]['eng'] == o['eng']:
                        continue
                    signaling.add(d[1])
        for d in final_waits:
            if d[0] == 'op':
                signaling.add(d[1])
        cnt = {e: 0 for e in ENGS}
        seq = {}
        for i, o in enumerate(ops):
            if o['dma'] is None and i in signaling:
                cnt[o['eng']] += 1
                seq[i] = cnt[o['eng']]
        stack = contextlib.ExitStack()
        esem = {e: stack.enter_context(nc.semaphore("s_" + e)) for e in ENGS}
        for s in self.boxes:
            s.sem = stack.enter_context(nc.semaphore("d%d" % s.id))
        per_eng = {e: [] for e in ENGS}
        waited = {e: {} for e in ENGS}
        nwaits = 0
        for i, o in enumerate(ops):
            e = o['eng']
            ws = {}
            for d in o['deps']:
                if d[0] in ('op', 'lop'):
                    de = ops[d[1]]['eng']
                    if de == 'pe' and e == 'pe':
                        continue
                    if d[0] == 'lop' and de == e:
                        continue
                    key = ('e', de)
                    val = seq[d[1]]
                    sem = esem[de]
                else:
                    key = ('d', d[1].id)
                    val = d[2]
                    sem = d[1].sem
                if waited[e].get(key, 0) >= val:
                    continue
                if key not in ws or ws[key][1] < val:
                    ws[key] = (sem, val)
            for key, (sem, val) in ws.items():
                waited[e][key] = val
            nwaits += len(ws)
            sig = None
            if o['dma'] is not None:
                sig = (o['dma'][0].sem, 16)
            elif i in signaling:
                sig = (esem[e], 1)
            per_eng[e].append((list(ws.values()), o['fn'], sig))
        fw = []
        for d in final_waits:
            if d[0] == 'op':
                fw.append((esem[ops[d[1]]['eng']], seq[d[1]]))
            else:
                fw.append((d[1].sem, d[2]))
        self.stats = dict(n_ops=len(ops), n_waits=nwaits, per_eng={e: len(per_eng[e]) for e in ENGS},
                          n_sems=len(self.boxes) + 5)

        def run(engine, lst, extra=()):
            for waits, fn, sig in lst:
                for sem, val in waits:
                    engine.wait_ge(sem, val)
                ins = fn(engine)
                if sig is not None:
                    ins.then_inc(sig[0], sig[1])
            for sem, val in extra:
                engine.wait_ge(sem, val)

        with nc.Block() as block:
            @block.tensor
            def _(eng):
                run(eng, per_eng['pe'])

            @block.scalar
            def _(eng):
                run(eng, per_eng['act'])

            @block.vector
            def _(eng):
                run(eng, per_eng['dve'])

            @block.gpsimd
            def _(eng):
                run(eng, per_eng['pool'])

            @block.sync
            def _(eng):
                run(eng, per_eng['sp'], fw)
        stack.close()


def _u_list(t):
    if t <= 1:
        return [0, 1, 2, 3]
    if t >= 14:
        return [12, 13, 14, 15]
    return [t - 2, t - 1, t, t + 1, t + 2]


def _tile_base(t):
    return {0: 5, 1: 9, 14: 13, 15: 17}.get(t, 0)


def _bias_index():
    tiles = [(8, u) for u in _u_list(8)]
    for t in (0, 1, 14, 15):
        tiles += [(t, u) for u in _u_list(t)]
    idx = np.zeros((21, 128, 128), np.int64)
    k = np.arange(128)
    q = np.arange(128)
    for ti, (t, u) in enumerate(tiles):
        krow = (2 * u + k // 64)[:, None]
        kcol = (k % 64)[:, None]
        qrow = (2 * t + q // 64)[None, :]
        qcol = (q % 64)[None, :]
        rs = np.clip(qrow - 4, 0, 24)
        cs = np.clip(qcol - 8, 0, 48)
        valid = (krow >= rs) & (krow < rs + 8) & (kcol >= cs) & (kcol < cs + 16)
        dr = krow - qrow
        dc = kcol - qcol
        ii = (dr + 7) * 31 + (dc + 15)
        idx[ti] = np.where(valid, ii, 465)
    return idx


def _dft_consts():
    bf = ml_dtypes.bfloat16
    c = np.arange(256, dtype=np.float64)
    th = 2 * np.pi * np.outer(c, c) / 256.0
    ccsc = np.concatenate([np.cos(th), np.sin(th)], axis=1) / 16.0
    ccsc = ccsc.reshape(2, 128, 512).transpose(1, 0, 2)
    out = {"ccsc": np.ascontiguousarray(ccsc).astype(bf)}
    for N, tag in ((SEQ, "L"), (CTX, "C")):
        n = np.arange(N, dtype=np.float64)
        th = 2 * np.pi * (np.outer(n, n) % N) / N
        out["cn" + tag] = (np.cos(th) / np.sqrt(N)).astype(bf)
        out["sn" + tag] = (-np.sin(th) / np.sqrt(N)).astype(bf)
    out["ident"] = np.eye(128, dtype=np.float32).astype(bf)
    return out


def build_program(layers=(0, 1), debug=False):
    nc = bass.Bass("TRN2", target_bir_lowering=False)
    P = Prog(nc)
    L = DEPTH

    def din(name, shape, dt=F32):
        return nc.dram_tensor(name, list(shape), dt, kind="ExternalInput").ap()

    def dscr(name, shape, dt=BF16):
        if debug:
            return nc.dram_tensor(name, list(shape), dt, kind="ExternalOutput").ap()
        return nc.dram_tensor(name, list(shape), dt).ap()

    x_in = din("x", [SEQ, D])
    ctx_in = din("ctx", [CTX, D])
    c2_in = din("c2", [128, 16, 2])
    normg_in = din("norm_g", [L, 128, 16])
    bada_in = din("b_ada", [L, 128, 48])
    wada_in = din("w_ada", [L, D, 3 * D])
    win_in = din("w_in", [L, D, W_IN])
    lng_in = din("ln_g", [L, 128, W_A])
    lnb_in = din("ln_b", [L, 128, W_A])
    wst_in = din("wsT", [L, 128, 8, 128])
    bs_in = din("bs", [L, 128, 8, 128])
    qkg_in = din("qkg", [128, L, 2])
    rpbt_in = din("rpbt", [L, 8, 128, 21, 128])
    wpa_in = din("w_pa", [L, W_A, D])
    wpf_in = din("w_pf", [L, W_A, D])
    wpn_in = din("w_pn", [L, W_A, D])
    wout_in = din("w_out", [L, D, D])
    ccsc_in = din("ccsc", [128, 2, 512], BF16)
    cnL_in = din("cnL", [SEQ, SEQ], BF16)
    snL_in = din("snL", [SEQ, SEQ], BF16)
    cnC_in = din("cnC", [CTX, CTX], BF16)
    snC_in = din("snC", [CTX, CTX], BF16)
    ident_in = din("ident", [128, 128], BF16)
    out_d = nc.dram_tensor("out", [SEQ, D], F32, kind="ExternalOutput").ap()

    zT = dscr("zT", [W_IN, NT])
    zVG = dscr("zVG", [NT, W_A])
    zV = dscr("zV", [NT, W_A])
    aT_d = dscr("aT", [W_A, NT])
    fT_d = dscr("fT", [W_A, NT])
    nT_d = dscr("nT", [W_A, NT])
    yT_d = dscr("yT", [D, NT])
    x1_d = dscr("x1", [SEQ, D], F32)
    ctx1_d = dscr("ctx1", [CTX, D], F32)
    gate_d = dscr("gate", [L, 2, D], F32)

    R = {k: Res(k) for k in ["zT", "zVG", "zV", "aT", "fT", "nT", "yT", "x1", "ctx1", "gate", "out", "const"]}
    RZ = {k: Res("zT_" + k) for k in ["U", "GA", "F", "GF", "Q", "K", "GN", "M"]}

    gstack = contextlib.ExitStack()

    def SB(stack, name, shape, dt):
        return stack.enter_context(nc.sbuf_tensor("sb_" + name, list(shape), dt))

    def PS(stack, name, shape, dt=F32):
        return stack.enter_context(nc.psum_tensor("ps_" + name, list(shape), dt))

    uid = [0]

    def nm(s):
        uid[0] += 1
        return "%s_%d" % (s, uid[0])

    ident = SB(gstack, "ident", [128, 128], BF16)
    r_ident = Res("ident")
    P.dma('sp', lambda e: e.dma_start(out=ident[:], in_=ident_in), r_ident, writes=[r_ident])
    ones_bf = SB(gstack, "ones_bf", [128, 128], BF16)
    r_ones = Res("ones")
    P.op('dve', lambda e: e.memset(ones_bf[:], 1.0), writes=[r_ones])
    qkg = SB(gstack, "qkg", [128, L, 2], F32)
    r_qkg = Res("qkg")
    P.dma('sp', lambda e: e.dma_start(out=qkg[:], in_=qkg_in), r_qkg, writes=[r_qkg])
    qkgs = SB(gstack, "qkgs", [128, L, 2], F32)
    r_qkgs = Res("qkgs")
    P.op('dve', lambda e: e.tensor_copy(out=qkgs[:], in_=qkg[:]), reads=[r_qkg], writes=[r_qkgs])
    for l_ in range(L):
        P.op('dve', lambda e, l_=l_: e.tensor_scalar(out=qkgs[:, l_, 0:1], in0=qkg[:, l_, 0:1], scalar1=float(128 ** -0.5),
                                                  scalar2=None, op0=ALU.mult), reads=[r_qkg, r_qkgs], writes=[r_qkgs])
    c2 = SB(gstack, "c2", [128, 16, 2], F32)
    r_c2 = Res("c2")
    P.dma('sp', lambda e: e.dma_start(out=c2[:], in_=c2_in), r_c2, writes=[r_c2])
    sc2 = SB(gstack, "sc2", [128, 16, 2], BF16)
    r_sc2 = Res("sc2")
    P.op('act', lambda e: e.activation(out=sc2[:], in_=c2[:], func=AF.Silu), reads=[r_c2], writes=[r_sc2])
    eps_t = SB(gstack, "eps_t", [128, 1], F32)
    r_eps = Res("eps")
    P.op('dve', lambda e: e.memset(eps_t[:], EPS), writes=[r_eps])

    TB = [(0, 512), (512, 512), (1024, 512), (1536, 512), (2048, 256)]

    modv_l = [SB(gstack, "modv%d" % i, [128, 48, 2], F32) for i in range(L)]
    r_modv_l = [Res("modv%d" % i) for i in range(L)]
    s1v_l = [SB(gstack, "s1v%d" % i, [128, 16, 2], F32) for i in range(L)]
    r_s1v_l = [Res("s1v%d" % i) for i in range(L)]
    bada_l = [SB(gstack, "bada%d" % i, [128, 48], F32) for i in range(L)]
    normg_l = [SB(gstack, "normg%d" % i, [128, 16], F32) for i in range(L)]
    r_bn_l = [Res("bn%d" % i) for i in range(L)]
    for i_ in range(L):
        r_t1_, r_t2_ = Res("bnl1"), Res("bnl2")
        P.dma('sp', lambda e, i_=i_: e.dma_start(out=bada_l[i_][:], in_=bada_in[i_]), r_t1_, writes=[r_bn_l[i_]])
        P.dma('sp', lambda e, i_=i_: e.dma_start(out=normg_l[i_][:], in_=normg_in[i_]), r_t2_, writes=[r_bn_l[i_]])

    def skew(n, phases):
        k = len(phases)
        for s in range(n + k - 1):
            for j in reversed(range(k)):
                i = s - j
                if 0 <= i < n:
                    phases[j](i)

    def mod_steps(lm, wa, r_wa, ps_mod, r_psmod):
        modv, r_modv, s1v, r_s1v = modv_l[lm], r_modv_l[lm], s1v_l[lm], r_s1v_l[lm]
        bada, normg, r_bn = bada_l[lm], normg_l[lm], r_bn_l[lm]
        wada_v = wada_in[lm].rearrange("(k p) c -> p k c", p=128)

        def ld_wa(i):
            b = i % 2
            P.dma('pool', lambda e: e.dma_start(out=wa[b][:], in_=wada_v[:, :, i * 512:(i + 1) * 512]), r_wa[b],
                  writes=[r_wa[b]])

        def step(i):
            if i == 0:
                ld_wa(0)
            if i + 1 < 12:
                ld_wa(i + 1)
            b = i % 2
            for j in range(4):
                cj = i * 4 + j
                for k in range(16):
                    P.op('pe', lambda e, b=b, j=j, k=k, cj=cj: e.matmul(ps_mod[:, cj, :], lhsT=wa[b][:, k, j * 128:(j + 1) * 128],
                                                                    rhs=sc2[:, k, :], start=(k == 0), stop=(k == 15)),
                         reads=[r_wa[b], r_sc2], writes=[r_psmod])

        r_modg = Res("modg")

        def post_a(_):
            for j in range(2):
                P.op('dve', lambda e, j=j: e.tensor_tensor(out=modv[:, 0:32, j], in0=ps_mod[:, 0:32, j], in1=bada[:, 0:32], op=ALU.add),
                     reads=[r_psmod, r_bn], pw=[r_modv])
            for j in range(2):
                P.op('dve', lambda e, j=j: e.scalar_tensor_tensor(out=s1v[:, :, j], in0=modv[:, 16:32, j], scalar=1.0, in1=normg[:],
                                                                op0=ALU.add, op1=ALU.mult),
                     reads=[r_modv, r_bn], pw=[r_s1v])

        def post_b(_):
            for j in range(2):
                P.op('dve', lambda e, j=j: e.tensor_tensor(out=modv[:, 32:48, j], in0=ps_mod[:, 32:48, j], in1=bada[:, 32:48], op=ALU.add),
                     reads=[r_psmod, r_bn], pw=[r_modg])
            r_gslot = Res("gslot")

            def gate_bounce(e, j):
                with nc.allow_non_contiguous_dma(reason="tiny gate vector bounce"):
                    return e.dma_start(out=gate_d[lm, j].rearrange("(k p) -> p k", p=128), in_=modv[:, 32:48, j])
            for j in range(2):
                P.dma('sp', lambda e, j=j: gate_bounce(e, j), r_gslot, reads=[r_modg], writes=[R["gate"]])
        return ([(lambda i=i: step(i)) for i in range(8)] + [lambda: post_a(0)] +
                [(lambda i=i: step(i)) for i in range(8, 12)] + [lambda: post_b(0)])

    def skew_steps_g(n, phases):
        k = len(phases)
        steps = []
        for s_ in range(n + k - 1):
            def stp(s_=s_):
                for j in reversed(range(k)):
                    i = s_ - j
                    if 0 <= i < n:
                        phases[j](i)
            steps.append(stp)
        return steps

    def gmlp_steps(st, l, TBL):
        lng = SB(st, nm("lng"), [128, W_A], F32)
        lnb = SB(st, nm("lnb"), [128, W_A], F32)
        bsb = SB(st, nm("bsb"), [128, 8, 128], F32)
        wsf = SB(st, nm("wsf"), [128, 8, 128], F32)
        wsb = SB(st, nm("wsb"), [128, 8, 128], BF16)
        r_ln = Res("ln")
        r_wsf = Res("wsf")
        r_wsb = Res("wsb")
        r_l1, r_l2, r_l3 = Res("l1"), Res("l2"), Res("l3")
        P.dma('sp', lambda e: e.dma_start(out=lng[:], in_=lng_in[l]), r_l1, pw=[r_ln])
        P.dma('sp', lambda e: e.dma_start(out=lnb[:], in_=lnb_in[l]), r_l2, pw=[r_ln])
        P.dma('sp', lambda e: e.dma_start(out=bsb[:], in_=bs_in[l]), r_l3, pw=[r_ln])
        P.dma('sp', lambda e: e.dma_start(out=wsf[:], in_=wst_in[l]), r_wsf, writes=[r_wsf])
        P.op('dve', lambda e: e.tensor_copy(out=wsb[:], in_=wsf[:]), reads=[r_wsf], writes=[r_wsb])
        vpre = [SB(st, nm("vpre"), [128, W_A], BF16) for _ in range(2)]
        r_vpre = [Res("vpre0"), Res("vpre1")]
        vf = SB(st, nm("vf"), [128, W_A], F32)
        r_vf = Res("vf")
        vf2 = SB(st, nm("vf2"), [128, W_A], F32)
        r_vf2 = Res("vf2")
        vb = [SB(st, nm("vb"), [128, W_A], BF16) for _ in range(2)]
        r_vb = [Res("vb0"), Res("vb1")]
        stt = [SB(st, nm("stt"), [128, 2, 6], F32) for _ in range(2)]
        mv = [SB(st, nm("mv"), [128, 4], F32) for _ in range(2)]
        r_mv = [Res("mv0"), Res("mv1")]
        UT = [SB(st, nm("UT"), [128, 8, 128], BF16) for _ in range(2)]
        GAT = [SB(st, nm("GAT"), [128, 8, 128], BF16) for _ in range(2)]
        r_UT = [Res("UT0"), Res("UT1")]
        r_GAT = [Res("GAT0"), Res("GAT1")]
        t1 = [SB(st, nm("t1"), [128, 8, 128], F32) for _ in range(2)]
        r_t1 = [Res("t10"), Res("t11")]
        t2 = SB(st, nm("t2"), [128, 8, 128], F32)
        r_t2 = Res("t2")
        ast = [SB(st, nm("ast"), [128, 8, 128], BF16) for _ in range(2)]
        r_ast = [Res("ast0"), Res("ast1")]
        psv = [PS(st, nm("psv"), [128, 8, 128]) for _ in range(2)]
        r_psv = [Res("psv0"), Res("psv1")]
        zU_v = zT[OFF_U:OFF_U + W_A, :].rearrange("(g p) t -> p g t", p=128)
        zGA_v = zT[OFF_GA:OFF_GA + W_A, :].rearrange("(g p) t -> p g t", p=128)
        aT_v = aT_d.rearrange("(g p) t -> p g t", p=128)
        toks = []
        for (t0, tn) in TBL:
            for ci in range(tn // 128):
                toks.append(t0 + ci * 128)

        def g_load(c):
            cb = c % 2
            tok0 = toks[c]
            P.dma('sp', lambda e: e.dma_start(out=vpre[cb][:], in_=zVG[tok0:tok0 + 128, :]), r_vpre[cb],
                  reads=[R["zVG"]], writes=[r_vpre[cb]])

        def g_p1(c):
            cb = c % 2
            for hh in range(2):
                P.op('dve', lambda e, hh=hh: e.bn_stats(out=stt[cb][:, hh, :], in_=vpre[cb][:, hh * 512:(hh + 1) * 512]),
                     reads=[r_vpre[cb]], writes=[r_mv[cb]])
            P.op('dve', lambda e: e.bn_aggr(out=mv[cb][:, 0:2], in_=stt[cb][:].rearrange("p a b -> p (a b)")),
                 reads=[r_mv[cb]], writes=[r_mv[cb]])
            P.op('act', lambda e: e.activation(out=mv[cb][:, 2:3], in_=mv[cb][:, 1:2], func=AF.Sqrt, scale=1.0, bias=eps_t[:]),
                 reads=[r_mv[cb], r_eps], writes=[r_mv[cb]])
            P.op('dve', lambda e: e.reciprocal(out=mv[cb][:, 3:4], in_=mv[cb][:, 2:3]), reads=[r_mv[cb]], writes=[r_mv[cb]])
            P.op('dve', lambda e: e.scalar_tensor_tensor(out=mv[cb][:, 2:3], in0=mv[cb][:, 0:1], scalar=-1.0, in1=mv[cb][:, 3:4],
                                                         op0=ALU.mult, op1=ALU.mult), reads=[r_mv[cb]], writes=[r_mv[cb]])
            P.op('act', lambda e: e.activation(out=vf[:], in_=vpre[cb][:], func=AF.Identity, scale=mv[cb][:, 3:4],
                                               bias=mv[cb][:, 2:3]), reads=[r_vpre[cb], r_mv[cb]], writes=[r_vf])

        def g_p2(c):
            cb = c % 2
            tok0 = toks[c]
            P.dma('sp', lambda e: e.dma_start(out=UT[cb][:], in_=zU_v[:, :, tok0:tok0 + 128]), r_UT[cb],
                  reads=[RZ["U"]], writes=[r_UT[cb]])
            P.dma('sp', lambda e: e.dma_start(out=GAT[cb][:], in_=zGA_v[:, :, tok0:tok0 + 128]), r_GAT[cb],
                  reads=[RZ["GA"]], writes=[r_GAT[cb]])
            P.op('pool', lambda e: e.tensor_tensor(out=vf2[:], in0=vf[:], in1=lng[:], op=ALU.mult), reads=[r_vf, r_ln],
                 writes=[r_vf2])
            P.op('pool', lambda e: e.tensor_tensor(out=vb[cb][:], in0=vf2[:], in1=lnb[:], op=ALU.add),
                 reads=[r_vf2, r_ln], writes=[r_vb[cb]])

        def g_p3(c):
            cb = c % 2
            for g in range(8):
                P.op('pe', lambda e, g=g: e.matmul(psv[cb][:, g, :], lhsT=vb[cb][:, g * 128:(g + 1) * 128], rhs=wsb[:, g, :],
                                                   start=True, stop=True),
                     reads=[r_vb[cb], r_wsb], writes=[r_psv[cb]])
            P.op('pool', lambda e: e.tensor_tensor(out=t1[cb][:], in0=UT[cb][:], in1=GAT[cb][:], op=ALU.mult),
                 reads=[r_UT[cb], r_GAT[cb]], writes=[r_t1[cb]])

        def g_p4(c):
            cb = c % 2
            tok0 = toks[c]
            P.op('dve', lambda e: e.tensor_tensor(out=t2[:], in0=psv[cb][:], in1=bsb[:], op=ALU.add),
                 reads=[r_psv[cb], r_ln], writes=[r_t2])
            P.op('dve', lambda e: e.tensor_tensor(out=ast[cb][:], in0=t1[cb][:], in1=t2[:], op=ALU.mult),
                 reads=[r_t1[cb], r_t2], writes=[r_ast[cb]])
            P.dma('sp', lambda e: e.dma_start(out=aT_v[:, :, tok0:tok0 + 128], in_=ast[cb][:]), r_ast[cb],
                  reads=[r_ast[cb]], pw=[R["aT"]])
        return skew_steps_g(len(toks), [g_load, g_p1, g_p2, g_p3, g_p4])

    def run_layer(l):
        last = (l == DEPTH - 1)
        xs_d, cs_d = (x_in, ctx_in) if l == 0 else (x1_d, ctx1_d)
        r_xs, r_cs = (R["const"], R["const"]) if l == 0 else (R["x1"], R["ctx1"])
        xo_d, r_xo = (out_d, R["out"]) if last else (x1_d, R["x1"])
        if len(layers) == 1:
            xo_d, r_xo = out_d, R["out"]
        modv, r_modv, s1v, r_s1v = modv_l[l], r_modv_l[l], s1v_l[l], r_s1v_l[l]

        hstack = contextlib.ExitStack()
        hT = SB(hstack, nm("hT"), [128, 16, NT], BF16)
        r_hT = Res("hT")

        P.barrier()
        st = contextlib.ExitStack()
        mtail = {}
        if l == layers[0]:
            wa = [SB(st, nm("wa"), [128, 16, 512], BF16) for _ in range(2)]
            r_wa = [Res("wa0"), Res("wa1")]
            ps_mod = PS(st, nm("ps_mod"), [128, 256, 2])
            r_psmod = Res("psmod")
            msteps0 = mod_steps(l, wa, r_wa, ps_mod, r_psmod)
            for f in msteps0[:9]:
                f()
            for i_, tt_ in enumerate((2, 6, 10, 14)):
                mtail[tt_] = msteps0[9 + i_]
        xt = [SB(st, nm("xt"), [128, D], F32) for _ in range(3)]
        r_xt = [Res("xt%d" % i) for i in range(3)]
        junk = SB(st, nm("junk"), [128, D], BF16)
        r_junk = Res("junk")
        xn = [SB(st, nm("xn"), [128, D], BF16) for _ in range(2)]
        r_xn = [Res("xn0"), Res("xn1")]
        ssq = [SB(st, nm("ssq"), [128, 4], F32) for _ in range(2)]
        r_ssq = [Res("ssq0"), Res("ssq1")]
        ps_tr = [PS(st, nm("ps_tr"), [128, 16, 128], BF16) for _ in range(2)]
        r_pstr = [Res("pstr0"), Res("pstr1")]

        def a_load(tt):
            b = tt % 3
            if tt < 16:
                src, rs = xs_d[tt * 128:(tt + 1) * 128, :], r_xs
            else:
                src, rs = cs_d[(tt - 16) * 128:(tt - 15) * 128, :], r_cs
            P.dma('sp', lambda e: e.dma_start(out=xt[b][:], in_=src), r_xt[b], reads=[rs], writes=[r_xt[b]])

        def a_p1(tt):
            b3, b = tt % 3, tt % 2
            P.op('act', lambda e: e.activation(out=junk[:], in_=xt[b3][:], func=AF.Square, accum_out=ssq[b][:, 0:1]),
                 reads=[r_xt[b3]], writes=[r_junk, r_ssq[b]])
            P.op('act', lambda e: e.activation(out=ssq[b][:, 1:2], in_=ssq[b][:, 0:1], func=AF.Sqrt, scale=1.0 / D, bias=eps_t[:]),
                 reads=[r_ssq[b], r_eps], writes=[r_ssq[b]])
            P.op('dve', lambda e: e.reciprocal(out=ssq[b][:, 2:3], in_=ssq[b][:, 1:2]), reads=[r_ssq[b]], writes=[r_ssq[b]])

        def a_p1b(tt):
            b3, b = tt % 3, tt % 2
            P.op('dve', lambda e: e.tensor_scalar(out=xn[b][:], in0=xt[b3][:], scalar1=ssq[b][:, 2:3], scalar2=None, op0=ALU.mult),
                 reads=[r_xt[b3], r_ssq[b]], writes=[r_xn[b]])

        def a_p2(tt):
            b = tt % 2
            if tt in mtail:
                mtail[tt]()
            for k in range(16):
                P.op('pe', lambda e, k=k: e.transpose(ps_tr[b][:, k, :], xn[b][:, k * 128:(k + 1) * 128], ident[:]),
                     reads=[r_xn[b], r_ident], writes=[r_pstr[b]])

        def a_p3(tt):
            b = tt % 2
            j = 0 if tt < 16 else 1
            for k in range(16):
                if k >= 8:
                    P.op('act', lambda e, k=k: e.activation(out=hT[:, k, tt * 128:(tt + 1) * 128], in_=ps_tr[b][:, k, :],
                                                            func=AF.Identity, scale=s1v[:, k, j:j + 1], bias=modv[:, k, j:j + 1]),
                         reads=[r_pstr[b], r_s1v, r_modv], pw=[r_hT])
                else:
                    P.op('dve', lambda e, k=k: e.tensor_scalar(out=hT[:, k, tt * 128:(tt + 1) * 128], in0=ps_tr[b][:, k, :],
                                                              scalar1=s1v[:, k, j:j + 1], scalar2=modv[:, k, j:j + 1],
                                                              op0=ALU.mult, op1=ALU.add),
                         reads=[r_pstr[b], r_s1v, r_modv], pw=[r_hT])
        skew(18, [a_load, a_p1, a_p1b, a_p2, a_p3])
        if l == layers[0]:
            msteps0[13]()
        P.barrier()
        st.close()
        if STOP == 'A':
            hstack.close()
            return

        st = contextlib.ExitStack()
        wb = [SB(st, nm("wb"), [128, 16, 512], BF16) for _ in range(2)]
        r_wb = [Res("wb0"), Res("wb1")]
        NZ = 3
        zst = [SB(st, nm("zst"), [128, NT], BF16) for _ in range(NZ)]
        r_zst = [Res("zst%d" % i) for i in range(NZ)]
        ztm = [SB(st, nm("ztm"), [128, 9, 512], BF16) for _ in range(2)]
        r_ztm = [Res("ztm0"), Res("ztm1")]
        pz = [PS(st, nm("pz"), [128, 512]) for _ in range(4)]
        r_pz = [Res("pz%d" % i) for i in range(4)]
        NTL = SEQ if last else NT
        TBL = [tb for tb in TB if tb[0] < NTL]
        g_steps = gmlp_steps(st, l, TBL)
        win_v = win_in[l].rearrange("(k p) c -> p k c", p=128)
        groups = ["U", "VG", "GA", "F", "GF", "Q", "K", "V", "GN"] + ["M"] * 6
        gfunc = {"U": AF.Gelu_apprx_tanh, "VG": AF.Gelu_apprx_tanh, "GA": AF.Silu, "GF": AF.Silu, "GN": AF.Silu,
                 "M": AF.Sigmoid}

        def ld_w(i):
            b = i % 2
            P.dma('pool', lambda e: e.dma_start(out=wb[b][:], in_=win_v[:, :, i * 512:(i + 1) * 512]), r_wb[b],
                  writes=[r_wb[b]])
        ld_w(0)
        cnts = dict(p=0, z=0, tm=0, cpy=0)

        def s2_tile(i):
            if i + 1 < 30:
                ld_w(i + 1)
            b = i % 2
            g = groups[i // 2]
            c0 = i * 512
            if g in ("VG", "V"):
                dst = zVG if g == "VG" else zV
                r_dst = R["zVG"] if g == "VG" else R["zV"]
                cc0 = c0 - (OFF_VG if g == "VG" else OFF_V)
                dst_v = dst[:, cc0:cc0 + 512].rearrange("(t p) c -> p t c", p=128)
                for hf in range(2):
                    tb_ = cnts['tm'] % 2
                    cnts['tm'] += 1
                    for t9 in range(9):
                        tt = hf * 9 + t9
                        pb = cnts['p'] % 4
                        cnts['p'] += 1
                        for k in range(16):
                            P.op('pe', lambda e, pb=pb, k=k, tt=tt: e.matmul(pz[pb][:], lhsT=hT[:, k, tt * 128:(tt + 1) * 128],
                                                                          rhs=wb[b][:, k, :], start=(k == 0), stop=(k == 15)),
                                 reads=[r_hT, r_wb[b]], writes=[r_pz[pb]])
                        if g == "VG":
                            P.op('act', lambda e, pb=pb, t9=t9, tb_=tb_: e.activation(out=ztm[tb_][:, t9, :], in_=pz[pb][:],
                                                                                    func=AF.Gelu_apprx_tanh),
                                 reads=[r_pz[pb]], pw=[r_ztm[tb_]])
                        else:
                            P.op('dve', lambda e, pb=pb, t9=t9, tb_=tb_: e.tensor_copy(out=ztm[tb_][:, t9, :], in_=pz[pb][:]),
                                 reads=[r_pz[pb]], pw=[r_ztm[tb_]])
                    P.dma('sp', lambda e, tb_=tb_, hf=hf: e.dma_start(out=dst_v[:, hf * 9:(hf + 1) * 9, :], in_=ztm[tb_][:]),
                          r_ztm[tb_], reads=[r_ztm[tb_]], pw=[r_dst])
                return
            for sbk in range(4):
                zb = cnts['z'] % NZ
                cnts['z'] += 1
                ntok = NT
                for (t0, tn) in TB:
                    if last and t0 >= SEQ and g != "K":
                        ntok = SEQ
                        continue
                    pb = cnts['p'] % 4
                    cnts['p'] += 1
                    for k in range(16):
                        P.op('pe', lambda e, pb=pb, k=k, t0=t0, tn=tn, sbk=sbk: e.matmul(
                            pz[pb][:, 0:tn], lhsT=wb[b][:, k, sbk * 128:(sbk + 1) * 128], rhs=hT[:, k, t0:t0 + tn],
                            start=(k == 0), stop=(k == 15)), reads=[r_hT, r_wb[b]], writes=[r_pz[pb]])
                    if g in gfunc:
                        P.op('act', lambda e, pb=pb, zb=zb, t0=t0, tn=tn, f=gfunc[g]: e.activation(
                            out=zst[zb][:, t0:t0 + tn], in_=pz[pb][:, 0:tn], func=f), reads=[r_pz[pb]], pw=[r_zst[zb]])
                    else:
                        cnts['cpy'] += 1
                        if cnts['cpy'] % 3 == 0:
                            P.op('act', lambda e, pb=pb, zb=zb, t0=t0, tn=tn: e.activation(
                                out=zst[zb][:, t0:t0 + tn], in_=pz[pb][:, 0:tn], func=AF.Copy), reads=[r_pz[pb]], pw=[r_zst[zb]])
                        else:
                            P.op('dve', lambda e, pb=pb, zb=zb, t0=t0, tn=tn: e.tensor_copy(
                                out=zst[zb][:, t0:t0 + tn], in_=pz[pb][:, 0:tn]), reads=[r_pz[pb]], pw=[r_zst[zb]])
                r0 = c0 + sbk * 128
                P.dma('sp', lambda e, zb=zb, r0=r0, ntok=ntok: e.dma_start(out=zT[r0:r0 + 128, 0:ntok], in_=zst[zb][:, 0:ntok]),
                      r_zst[zb], reads=[r_zst[zb]], pw=[RZ[g]])

        for i in range(30):
            s2_tile(i)
            if i >= 6 and g_steps:
                g_steps.pop(0)()
        while g_steps:
            g_steps.pop(0)()
        P.barrier()
        st.close()
        hstack.close()
        if STOP == 'S2':
            return

        st = contextlib.ExitStack()
        ccsc = SB(st, nm("ccsc"), [128, 2, 512], BF16)
        r_ccsc = Res("ccsc")
        P.dma('sp', lambda e: e.dma_start(out=ccsc[:], in_=ccsc_in), r_ccsc, writes=[r_ccsc])
        XCS = SB(st, nm("XCS"), [128, 16, 4, 512], BF16)
        r_XCS = Res("XCS")
        zFT = [SB(st, nm("zFT"), [128, 2, SEQ], BF16) for _ in range(2)]
        r_zFT = [Res("zFT0"), Res("zFT1")]
        cnb = [SB(st, nm("cnb"), [128, 16, 512], BF16) for _ in range(2)]
        snb = [SB(st, nm("snb"), [128, 16, 512], BF16) for _ in range(2)]
        r_cnb = [Res("cnb0"), Res("cnb1")]
        r_snb = [Res("snb0"), Res("snb1")]
        GFT = [SB(st, nm("GFT"), [128, 8, 512], BF16) for _ in range(2)]
        r_GFT = [Res("GFT0"), Res("GFT1")]
        fst = [SB(st, nm("fst"), [128, 8, 512], BF16) for _ in range(2)]
        r_fst = [Res("fst0"), Res("fst1")]
        pf1 = [PS(st, nm("pf1"), [128, 512]) for _ in range(4)]
        r_pf1 = [Res("pf1%d" % i) for i in range(4)]
        pf2 = [PS(st, nm("pf2"), [128, 512]) for _ in range(2)]
        r_pf2 = [Res("pf20"), Res("pf21")]
        zF_v = zT[OFF_F:OFF_F + 1024, :].rearrange("(G j p) t -> G p j t", j=2, p=128)
        zGF_v = zT[OFF_GF:OFF_GF + 1024, :].rearrange("(c p) t -> p c t", p=128)
        fT_v = fT_d.rearrange("(c p) t -> p c t", p=128)
        fcnt = dict(z=0, p1=0, c=0, p2=0)
        jobs = [(SEQ, 0, cnL_in, snL_in)] + ([] if last else [(CTX, SEQ, cnC_in, snC_in)])
        s1_units = [(N, tok0, G) for (N, tok0, _, _) in jobs for G in range(4)]
        s2_units = [(N, tok0, cn_d, sn_d, nb) for (N, tok0, cn_d, sn_d) in jobs for nb in range(max(1, N // 512))]

        def f_ld1(i):
            if i >= len(s1_units):
                return
            N, tok0, G = s1_units[i]
            zb = i % 2
            P.dma('sp', lambda e: e.dma_start(out=zFT[zb][:, :, 0:N], in_=zF_v[G][:, :, tok0:tok0 + N]), r_zFT[zb],
                  reads=[RZ["F"]], writes=[r_zFT[zb]])

        def f_ld2(i):
            if i >= len(s2_units):
                return
            N, tok0, cn_d, sn_d, nb = s2_units[i]
            T = N // 128
            nbw = min(512, N)
            cb = i % 2
            cn_v = cn_d.rearrange("(t p) c -> p t c", p=128)
            sn_v = sn_d.rearrange("(t p) c -> p t c", p=128)
            P.dma('sp', lambda e: e.dma_start(out=cnb[cb][:, 0:T, 0:nbw], in_=cn_v[:, :, nb * nbw:(nb + 1) * nbw]),
                  r_cnb[cb], reads=[R["const"]], writes=[r_cnb[cb]])
            P.dma('sp', lambda e: e.dma_start(out=snb[cb][:, 0:T, 0:nbw], in_=sn_v[:, :, nb * nbw:(nb + 1) * nbw]),
                  r_snb[cb], reads=[R["const"]], writes=[r_snb[cb]])
            c0 = tok0 + nb * nbw
            P.dma('sp', lambda e: e.dma_start(out=GFT[cb][:, :, 0:nbw], in_=zGF_v[:, :, c0:c0 + nbw]), r_GFT[cb],
                  reads=[RZ["GF"]], writes=[r_GFT[cb]])

        f_ld1(0)
        f_ld2(0)
        i2c = [0]

        def f_job(ji, N, tok0, cn_d, sn_d):
                T = N // 128
                nbw = min(512, N)
                for G in range(4):
                    i1 = ji * 4 + G
                    zb = i1 % 2
                    f_ld1(i1 + 1)
                    for t in range(T):
                        pb = fcnt['p1'] % 4
                        fcnt['p1'] += 1
                        for j in range(2):
                            P.op('pe', lambda e, pb=pb, zb=zb, j=j, t=t: e.matmul(pf1[pb][:], lhsT=zFT[zb][:, j, t * 128:(t + 1) * 128],
                                                                               rhs=ccsc[:, j, :], start=(j == 0), stop=(j == 1)),
                                 reads=[r_zFT[zb], r_ccsc], writes=[r_pf1[pb]])
                        if t % 2 == 0:
                            P.op('dve', lambda e, pb=pb, t=t, G=G: e.tensor_copy(out=XCS[:, t, G, :], in_=pf1[pb][:]),
                                 reads=[r_pf1[pb]], pw=[r_XCS])
                        else:
                            P.op('act', lambda e, pb=pb, t=t, G=G: e.activation(out=XCS[:, t, G, :], in_=pf1[pb][:], func=AF.Copy),
                                 reads=[r_pf1[pb]], pw=[r_XCS])
                for nb in range(N // nbw):
                    cb = i2c[0] % 2
                    i2c[0] += 1
                    f_ld2(i2c[0])
                    c0 = tok0 + nb * nbw
                    for G in range(4):
                        for j in range(2):
                            pb = fcnt['p2'] % 2
                            fcnt['p2'] += 1
                            for t in range(T):
                                P.op('pe', lambda e, pb=pb, t=t, G=G, j=j, cb=cb: e.matmul(
                                    pf2[pb][:, 0:nbw], lhsT=XCS[:, t, G, j * 128:(j + 1) * 128], rhs=cnb[cb][:, t, 0:nbw],
                                    start=(t == 0), stop=False), reads=[r_XCS, r_cnb[cb]], writes=[r_pf2[pb]])
                            for t in range(T):
                                P.op('pe', lambda e, pb=pb, t=t, G=G, j=j, cb=cb: e.matmul(
                                    pf2[pb][:, 0:nbw], lhsT=XCS[:, t, G, 256 + j * 128:256 + (j + 1) * 128], rhs=snb[cb][:, t, 0:nbw],
                                    start=False, stop=(t == T - 1)), reads=[r_XCS, r_snb[cb]], writes=[r_pf2[pb]])
                            cch = G * 2 + j
                            P.op('dve', lambda e, pb=pb, cb=cb, cch=cch: e.tensor_tensor(out=fst[cb][:, cch, 0:nbw], in0=pf2[pb][:, 0:nbw],
                                                                                         in1=GFT[cb][:, cch, 0:nbw], op=ALU.mult),
                                 reads=[r_pf2[pb], r_GFT[cb]], pw=[r_fst[cb]])
                    P.dma('sp', lambda e, cb=cb, c0=c0: e.dma_start(out=fT_v[:, :, c0:c0 + nbw], in_=fst[cb][:, :, 0:nbw]), r_fst[cb],
                          reads=[r_fst[cb]], writes=[R["fT"]])
        for ji, (N, tok0, cn_d, sn_d) in enumerate(jobs):
            f_job(ji, N, tok0, cn_d, sn_d)
        P.barrier()
        st.close()

        if STOP == 'S3b':
            return
        wstack = contextlib.ExitStack()
        wp = [SB(wstack, nm("wp"), [128, 8, D], BF16) for _ in range(3)]
        r_wp = [[Res("wp%d%d" % (i, hf)) for hf in range(2)] for i in range(3)]
        for i, wsrc in enumerate((wpa_in, wpf_in, wpn_in)):
            for hf in range(2):
                P.dma('pool', lambda e, i=i, wsrc=wsrc, hf=hf: e.dma_start(
                    out=wp[i][:, :, hf * 1024:(hf + 1) * 1024],
                    in_=wsrc[l].rearrange("(k p) c -> p k c", p=128)[:, :, hf * 1024:(hf + 1) * 1024]), r_wp[i][hf],
                    writes=[r_wp[i][hf]])
        st = contextlib.ExitStack()
        QT1 = SB(st, nm("QT"), [128, NT], BF16)
        KT1 = SB(st, nm("KT"), [128, NT], BF16)
        QT = [QT1, QT1]
        KT = [KT1, KT1]
        r_QT1, r_KT1 = Res("QT"), Res("KT")
        r_QT = [r_QT1, r_QT1]
        r_KT = [r_KT1, r_KT1]
        QN = [SB(st, nm("QN"), [128, NT], BF16) for _ in range(2)]
        KN = [SB(st, nm("KN"), [128, NT], BF16) for _ in range(2)]
        r_QN = [Res("QN0"), Res("QN1")]
        r_KN = [Res("KN0"), Res("KN1")]
        V1 = [SB(st, nm("V1"), [128, 18, 132], BF16) for _ in range(2)]
        r_V1 = [Res("V10"), Res("V11")]
        for b in range(2):
            P.op('dve', lambda e, b=b: e.memset(V1[b][:], 1.0), writes=[r_V1[b]])
        GNT = [SB(st, nm("GNT"), [128, NT], BF16) for _ in range(2)]
        r_GNT = [Res("GNT0"), Res("GNT1")]
        BT = [SB(st, nm("BT"), [128, 21, 128], F32) for _ in range(2)]
        r_BT = [Res("BT0"), Res("BT1")]
        nst = [SB(st, nm("nst"), [128, NT], BF16) for _ in range(2)]
        r_nst = [Res("nst0"), Res("nst1")]
        sq = [SB(st, nm("sq"), [128, 512], BF16) for _ in range(2)]
        r_sq = [Res("sq0"), Res("sq1")]
        lnt = [SB(st, nm("lnt"), [128, 512], F32) for _ in range(2)]
        r_lnt = [Res("lnt0"), Res("lnt1")]
        rt = [SB(st, nm("rt"), [128, 512], F32) for _ in range(2)]
        r_rt = [Res("rt0"), Res("rt1")]
        Sb = [SB(st, nm("Sb"), [128, 640], F32) for _ in range(2)]
        r_Sb = [Res("Sb0"), Res("Sb1")]
        Eb = [SB(st, nm("Eb"), [128, 896], BF16) for _ in range(2)]
        r_Eb = [Res("Eb0"), Res("Eb1")]
        on = [SB(st, nm("on"), [128, 128], BF16) for _ in range(2)]
        r_on = [Res("on0"), Res("on1")]
        rd = [SB(st, nm("rd"), [128, 2], F32) for _ in range(2)]
        r_rd = [Res("rd0"), Res("rd1")]
        pss = [PS(st, nm("pss"), [128, 1024]) for _ in range(2)]
        r_pss = [Res("pss0"), Res("pss1")]
        bk = [PS(st, nm("bk"), [128, 512]) for _ in range(2)]
        r_bk = [Res("bk0"), Res("bk1")]
        for r_ in r_bk:
            r_.lock = True
        pso_v = [bk[i][:, 0:129] for i in range(2)]
        pst_v = [bk[i][:, 256:320].bitcast(BF16) for i in range(2)]
        psn = [PS(st, nm("psn"), [128, 512]) for _ in range(2)]
        r_psn = [Res("psn0"), Res("psn1")]
        nq_tiles = 16 if last else 18

        def c_loads_a(h):
            if h >= 8:
                return
            hb = h % 2
            P.dma('sp', lambda e: e.dma_start(out=QT[hb][:, 0:NTL], in_=zT[OFF_Q + h * 128:OFF_Q + (h + 1) * 128, 0:NTL]),
                  r_QT[hb], reads=[RZ["Q"]], writes=[r_QT[hb]])
            P.dma('sp', lambda e: e.dma_start(out=KT[hb][:], in_=zT[OFF_K + h * 128:OFF_K + (h + 1) * 128, :]),
                  r_KT[hb], reads=[RZ["K"]], writes=[r_KT[hb]])

        def c_loads(h):
            if h >= 8:
                return
            hb = h % 2
            P.dma('sp', lambda e: e.dma_start(out=V1[hb][:, :, 0:128],
                                              in_=zV[:, h * 128:(h + 1) * 128].rearrange("(t p) d -> p t d", p=128)),
                  r_V1[hb], reads=[R["zV"]], writes=[r_V1[hb]])
            P.dma('sp', lambda e: e.dma_start(out=GNT[hb][:, 0:NTL], in_=zT[OFF_GN + h * 128:OFF_GN + (h + 1) * 128, 0:NTL]),
                  r_GNT[hb], reads=[RZ["GN"]], writes=[r_GNT[hb]])
            P.dma('sp', lambda e: e.dma_start(out=BT[hb][:], in_=rpbt_in[l, h]), r_BT[hb], reads=[R["const"]],
                  writes=[r_BT[hb]])

        def norm_phases(h):
            hb = h % 2
            blocks = []
            for (src, r_src, dstn, r_dstn, gi, ntk) in ((QT[hb], r_QT[hb], QN[hb], r_QN[hb], 0, NTL),
                                                        (KT[hb], r_KT[hb], KN[hb], r_KN[hb], 1, NT)):
                for (t0, tn) in TB:
                    if t0 < ntk:
                        blocks.append((src, r_src, dstn, r_dstn, gi, t0, tn))

            def n0(i):
                src, r_src, dstn, r_dstn, gi, t0, tn = blocks[i]
                nb_ = i % 2
                P.op('pool', lambda e: e.tensor_tensor(out=sq[nb_][:, 0:tn], in0=src[:, t0:t0 + tn], in1=src[:, t0:t0 + tn], op=ALU.mult),
                     reads=[r_src], writes=[r_sq[nb_]])

            def n1(i):
                src, r_src, dstn, r_dstn, gi, t0, tn = blocks[i]
                nb_ = i % 2
                P.op('pe', lambda e: e.matmul(psn[nb_][:, 0:tn], lhsT=ones_bf[:], rhs=sq[nb_][:, 0:tn], start=True, stop=True),
                     reads=[r_sq[nb_], r_ones], writes=[r_psn[nb_]])

            def n2(i):
                src, r_src, dstn, r_dstn, gi, t0, tn = blocks[i]
                nb_ = i % 2
                P.op('act', lambda e: e.activation(out=lnt[nb_][:, 0:tn], in_=psn[nb_][:, 0:tn], func=AF.Ln, scale=1.0 / 128.0,
                                                   bias=eps_t[:]), reads=[r_psn[nb_], r_eps], writes=[r_lnt[nb_]])
                P.op('act', lambda e: e.activation(out=rt[nb_][:, 0:tn], in_=lnt[nb_][:, 0:tn], func=AF.Exp, scale=-0.5),
                     reads=[r_lnt[nb_]], writes=[r_rt[nb_]])

            def n3(i):
                src, r_src, dstn, r_dstn, gi, t0, tn = blocks[i]
                nb_ = i % 2
                P.op('dve', lambda e: e.scalar_tensor_tensor(out=dstn[:, t0:t0 + tn], in0=src[:, t0:t0 + tn], scalar=qkgs[:, l, gi:gi + 1],
                                                             in1=rt[nb_][:, 0:tn], op0=ALU.mult, op1=ALU.mult),
                     reads=[r_src, r_rt[nb_], r_qkgs], pw=[r_dstn])
            return len(blocks), [n0, n1, n2, n3]

        def skew_steps(n, phases):
            k = len(phases)
            steps = []
            for s in range(n + k - 1):
                def stp(s=s):
                    for j in reversed(range(k)):
                        i = s - j
                        if 0 <= i < n and phases[j] is not None:
                            phases[j](i)
                steps.append(stp)
            return steps

        units = [(h, t) for h in range(8) for t in range(nq_tiles)]

        def geom(u):
            h, t = units[u]
            if t < 16:
                us = _u_list(t)
                base = _tile_base(t)
            else:
                us, base = [], 0
            chunks_ = [uu * 128 for uu in us] + [SEQ, SEQ + 128]
            vt = list(us) + [16, 17]
            return h, t, h % 2, us, base, chunks_, vt, len(us), len(chunks_), t * 128

        pending = {'front': [], 'back': []}

        def c_p0(u):
            h, t, hb, us, base, chunks_, vt, nloc, nch, q0 = geom(u)
            ub = u % 2
            if t == 0 and h + 1 < 8:
                c_loads_a(h + 1)
                nbk, ph = norm_phases(h + 1)
                pending['front'] = skew_steps(nbk, [None, None, ph[2], ph[3]])
                pending['back'] = skew_steps(nbk, [ph[0], ph[1], None, None])
            if t == 6:
                c_loads(h + 1)
            for i, k0 in enumerate(chunks_):
                P.op('pe', lambda e, i=i, k0=k0: e.matmul(pss[ub][:, i * 128:(i + 1) * 128], lhsT=KN[hb][:, k0:k0 + 128],
                                                         rhs=QN[hb][:, q0:q0 + 128], start=True, stop=True),
                     reads=[r_KN[hb], r_QN[hb]], writes=[r_pss[ub]])
            if pending['back']:
                pending['back'].pop(0)()

        def c_p1(u):
            h, t, hb, us, base, chunks_, vt, nloc, nch, q0 = geom(u)
            ub = u % 2
            if nloc > 0:
                P.op('dve', lambda e: e.tensor_tensor(out=Sb[ub][:, 0:nloc * 128], in0=pss[ub][:, 0:nloc * 128],
                                                      in1=BT[hb][:, base:base + nloc, :].rearrange("p a b -> p (a b)"), op=ALU.add),
                     reads=[r_pss[ub], r_BT[hb]], writes=[r_Sb[ub]])
                P.op('act', lambda e: e.activation(out=Eb[ub][:, 0:nloc * 128], in_=Sb[ub][:, 0:nloc * 128], func=AF.Exp),
                     reads=[r_Sb[ub]], pw=[r_Eb[ub]])
            P.op('act', lambda e: e.activation(out=Eb[ub][:, nloc * 128:(nloc + 2) * 128], in_=pss[ub][:, nloc * 128:(nloc + 2) * 128],
                                               func=AF.Exp), reads=[r_pss[ub]], pw=[r_Eb[ub]])

        def c_p2(u):
            h, t, hb, us, base, chunks_, vt, nloc, nch, q0 = geom(u)
            ub = u % 2
            for i in range(nch):
                P.op('pe', lambda e, i=i: e.matmul(pso_v[ub], lhsT=Eb[ub][:, i * 128:(i + 1) * 128], rhs=V1[hb][:, vt[i], 0:129],
                                                   start=(i == 0), stop=(i == nch - 1)),
                     reads=[r_Eb[ub], r_V1[hb]], writes=[r_bk[ub]])

        def c_p3(u):
            ub = u % 2
            P.op('dve', lambda e: e.reciprocal(out=rd[ub][:, 0:1], in_=bk[ub][:, 128:129]), writes=[r_bk[ub], r_rd[ub]])
            P.op('dve', lambda e: e.tensor_scalar(out=on[ub][:], in0=bk[ub][:, 0:128], scalar1=rd[ub][:, 0:1], scalar2=None, op0=ALU.mult),
                 reads=[r_rd[ub]], writes=[r_bk[ub], r_on[ub]])

        def c_p4(u):
            ub = u % 2
            P.op('pe', lambda e: e.transpose(pst_v[ub], on[ub][:], ident[:]), reads=[r_on[ub], r_ident], writes=[r_bk[ub]])

        def c_pn(u):
            if pending['front']:
                pending['front'].pop(0)()

        def c_p5(u):
            h, t, hb, us, base, chunks_, vt, nloc, nch, q0 = geom(u)
            ub = u % 2
            P.op('dve', lambda e: e.tensor_tensor(out=nst[hb][:, q0:q0 + 128], in0=pst_v[ub], in1=GNT[hb][:, q0:q0 + 128],
                                                  op=ALU.mult), reads=[r_GNT[hb]], writes=[r_bk[ub]], pw=[r_nst[hb]])
            if t == nq_tiles - 1:
                P.dma('sp', lambda e: e.dma_start(out=nT_d[h * 128:(h + 1) * 128, 0:NTL], in_=nst[hb][:, 0:NTL]), r_nst[hb],
                      reads=[r_nst[hb]], writes=[R["nT"]])

        c_loads_a(0)
        c_loads(0)
        nbk0, ph0 = norm_phases(0)
        skew(nbk0, ph0)
        skew(len(units), [c_p0, c_p1, c_p2, c_p3, c_p4, c_p5, c_pn])
        P.barrier()
        st.close()
        if STOP == 'S3c':
            wstack.close()
            return

        st = contextlib.ExitStack()
        br = [[SB(st, nm("br"), [128, 8, 512], BF16) for _ in range(3)] for _ in range(2)]
        r_br = [[Res("br") for _ in range(3)] for _ in range(2)]
        gt = [SB(st, nm("gt"), [128, 3, 512], BF16) for _ in range(3)]
        r_gt = [Res("gt%d" % i) for i in range(3)]
        yy = [[SB(st, nm("yy"), [128, 512], F32) for _ in range(3)] for _ in range(2)]
        r_yy = [[Res("yy") for _ in range(3)] for _ in range(2)]
        yst = [SB(st, nm("yst"), [128, 16, 512], BF16) for _ in range(2)]
        r_yst = [Res("yst0"), Res("yst1")]
        pp = [PS(st, nm("pp"), [128, 512]) for _ in range(6)]
        r_pp = [Res("pp%d" % i) for i in range(6)]
        br_v = [d_.rearrange("(k p) t -> p k t", p=128) for d_ in (aT_d, fT_d, nT_d)]
        r_brd = [R["aT"], R["fT"], R["nT"]]
        zM_v = zT[OFF_MERGE:OFF_MERGE + 3 * D, :].rearrange("(j c p) t -> c p j t", j=3, c=16, p=128)
        yT_v = yT_d.rearrange("(c p) t -> p c t", p=128)
        units4 = [(bi, dc) for bi in range(len(TBL)) for dc in range(16)]

        def d_brload(bi):
            if bi >= len(TBL):
                return
            t0, tn = TBL[bi]
            bb = bi % 2
            for i in range(3):
                P.dma('sp', lambda e, i=i: e.dma_start(out=br[bb][i][:, :, 0:tn], in_=br_v[i][:, :, t0:t0 + tn]), r_br[bb][i],
                      reads=[r_brd[i]], writes=[r_br[bb][i]])
        d_brload(0)

        def d_load(u):
            bi, dc = units4[u]
            t0, tn = TBL[bi]
            gb = u % 3
            if dc == 2:
                d_brload(bi + 1)
            P.dma('sp', lambda e: e.dma_start(out=gt[gb][:, :, 0:tn], in_=zM_v[dc][:, :, t0:t0 + tn]), r_gt[gb],
                  reads=[RZ["M"]], writes=[r_gt[gb]])

        def d_p1(u):
            bi, dc = units4[u]
            t0, tn = TBL[bi]
            bb, pb = bi % 2, u % 2
            for i in range(3):
                pi = pb * 3 + i
                for k in range(8):
                    P.op('pe', lambda e, pi=pi, i=i, k=k: e.matmul(pp[pi][:, 0:tn], lhsT=wp[i][:, k, dc * 128:(dc + 1) * 128],
                                                                  rhs=br[bb][i][:, k, 0:tn], start=(k == 0), stop=(k == 7)),
                         reads=[r_wp[i][dc // 8], r_br[bb][i]], writes=[r_pp[pi]])

        def d_p2(u):
            bi, dc = units4[u]
            t0, tn = TBL[bi]
            bb, pb, gb = bi % 2, u % 2, u % 3
            for i in range(3):
                pi = pb * 3 + i
                P.op('dve', lambda e, pi=pi, i=i: e.tensor_tensor(out=yy[pb][i][:, 0:tn], in0=pp[pi][:, 0:tn], in1=gt[gb][:, i, 0:tn],
                                                                  op=ALU.mult),
                     reads=[r_pp[pi], r_gt[gb]], writes=[r_yy[pb][i]])
            P.op('pool', lambda e: e.tensor_tensor(out=yy[pb][0][:, 0:tn], in0=yy[pb][0][:, 0:tn], in1=yy[pb][1][:, 0:tn], op=ALU.add),
                 reads=[r_yy[pb][0], r_yy[pb][1]], writes=[r_yy[pb][0]])
            P.op('pool', lambda e: e.tensor_tensor(out=yst[bb][:, dc, 0:tn], in0=yy[pb][0][:, 0:tn], in1=yy[pb][2][:, 0:tn], op=ALU.add),
                 reads=[r_yy[pb][0], r_yy[pb][2]], pw=[r_yst[bb]])
            if dc == 15:
                P.dma('sp', lambda e: e.dma_start(out=yT_v[:, :, t0:t0 + tn], in_=yst[bb][:, :, 0:tn]), r_yst[bb], reads=[r_yst[bb]],
                      writes=[R["yT"]])
        skew(len(units4), [d_load, d_p1, d_p2])
        P.barrier()
        st.close()
        wstack.close()
        if STOP == 'S4a':
            return

        st = contextlib.ExitStack()
        wo = SB(st, nm("wo"), [128, 16, D], BF16)
        r_wo = [Res("wo%d" % i) for i in range(4)]
        for hf in range(4):
            P.dma('pool', lambda e, hf=hf: e.dma_start(out=wo[:, :, hf * 512:(hf + 1) * 512],
                                                      in_=wout_in[l].rearrange("(k p) c -> p k c", p=128)[:, :, hf * 512:(hf + 1) * 512]),
                  r_wo[hf], writes=[r_wo[hf]])
        gbc = SB(st, nm("gbc"), [128, 2, D], F32)
        r_gbc = Res("gbc")
        r_gb2 = Res("gb2")
        P.dma('sp', lambda e: e.dma_start(out=gbc[:, 0, :], in_=gate_d[l, 0].partition_broadcast(128)), r_gbc, reads=[R["gate"]],
              writes=[r_gbc])
        P.dma('sp', lambda e: e.dma_start(out=gbc[:, 1, :], in_=gate_d[l, 1].partition_broadcast(128)), r_gb2, reads=[R["gate"]],
              writes=[r_gbc])
        yt_ = [SB(st, nm("yt"), [128, 16, 512], BF16) for _ in range(2)]
        r_yt = [Res("yt0"), Res("yt1")]
        xr = [SB(st, nm("xr"), [128, D], F32) for _ in range(3)]
        r_xr = [Res("xr%d" % i) for i in range(3)]
        xo = [SB(st, nm("xo"), [128, D], F32) for _ in range(2)]
        r_xo_s = [Res("xo0"), Res("xo1")]
        tg = [SB(st, nm("tg"), [128, 512], F32) for _ in range(2)]
        r_tg = [Res("tg0"), Res("tg1")]
        po = [PS(st, nm("po"), [128, 512]) for _ in range(4)]
        r_po = [Res("po%d" % i) for i in range(4)]
        msteps = []
        if (not last) and (l + 1) in layers:
            wa = [SB(st, nm("wa"), [128, 16, 512], BF16) for _ in range(2)]
            r_wa = [Res("wa0"), Res("wa1")]
            ps_mod = PS(st, nm("ps_mod"), [128, 256, 2])
            r_psmod = Res("psmod")
            msteps = mod_steps(l + 1, wa, r_wa, ps_mod, r_psmod)
        tiles = []
        for bi, (t0, tn) in enumerate(TBL):
            for ti in range(tn // 128):
                tiles.append((bi, t0, tn, ti))
        units5 = [(tix, db) for tix in range(len(tiles)) for db in range(4)]

        def e_ytload(bi):
            if bi >= len(TBL):
                return
            t0, tn = TBL[bi]
            yb = bi % 2
            P.dma('sp', lambda e: e.dma_start(out=yt_[yb][:, :, 0:tn], in_=yT_v[:, :, t0:t0 + tn]), r_yt[yb],
                  reads=[R["yT"]], writes=[r_yt[yb]])
        e_ytload(0)

        def tile_io(tix):
            bi, t0, tn, ti = tiles[tix]
            tok0 = t0 + ti * 128
            if tok0 >= SEQ:
                return (cs_d[tok0 - SEQ:tok0 - SEQ + 128, :], r_cs, ctx1_d[tok0 - SEQ:tok0 - SEQ + 128, :], R["ctx1"], 1)
            return (xs_d[tok0:tok0 + 128, :], r_xs, xo_d[tok0:tok0 + 128, :], r_xo, 0)

        def e_load(u):
            tix, db = units5[u]
            bi, t0, tn, ti = tiles[tix]
            if ti == 0 and db == 2:
                e_ytload(bi + 1)
            if db != 0:
                return
            src, rs, dst, rdst, j = tile_io(tix)
            xb = tix % 3
            P.dma('sp', lambda e: e.dma_start(out=xr[xb][:], in_=src), r_xr[xb], reads=[rs], writes=[r_xr[xb]])

        def e_p1(u):
            tix, db = units5[u]
            bi, t0, tn, ti = tiles[tix]
            yb, pb = bi % 2, u % 4
            if msteps and u % 4 == 0:
                msteps.pop(0)()
            for k in range(16):
                P.op('pe', lambda e, k=k: e.matmul(po[pb][:], lhsT=yt_[yb][:, k, ti * 128:(ti + 1) * 128],
                                                   rhs=wo[:, k, db * 512:(db + 1) * 512], start=(k == 0), stop=(k == 15)),
                     reads=[r_yt[yb], r_wo[db]], writes=[r_po[pb]])

        def e_p2(u):
            tix, db = units5[u]
            src, rs, dst, rdst, j = tile_io(tix)
            pb, tb_, xb3, xb = u % 4, u % 2, tix % 3, tix % 2
            P.op('dve', lambda e: e.tensor_tensor(out=tg[tb_][:], in0=po[pb][:], in1=gbc[:, j, db * 512:(db + 1) * 512], op=ALU.mult),
                 reads=[r_po[pb], r_gbc], writes=[r_tg[tb_]])
            P.op('pool', lambda e: e.tensor_tensor(out=xo[xb][:, db * 512:(db + 1) * 512], in0=tg[tb_][:],
                                                   in1=xr[xb3][:, db * 512:(db + 1) * 512], op=ALU.add),
                 reads=[r_tg[tb_], r_xr[xb3]], pw=[r_xo_s[xb]])
            if db == 3:
                P.dma('sp', lambda e: e.dma_start(out=dst, in_=xo[xb][:]), r_xo_s[xb], reads=[r_xo_s[xb]], writes=[rdst])
        skew(len(units5), [e_load, e_p1, e_p2])
        while msteps:
            msteps.pop(0)()
        P.barrier()
        st.close()

    for l_i in layers:
        run_layer(l_i)
    P.emit(final_waits=P.all_tokens())
    gstack.close()
    return nc, P


_CACHE = {}


def _prep_inputs(inp):
    f32 = np.float32
    L = DEPTH
    consts = _dft_consts()
    idx = _bias_index()
    rpb = np.asarray(inp["rpb"], f32).reshape(L, 8, 15 * 31)
    tab = np.concatenate([rpb, np.full((L, 8, 1), NEG, f32)], axis=2)
    rpbt = tab[:, :, idx]
    rpbt = np.ascontiguousarray(rpbt.transpose(0, 1, 3, 2, 4))
    shared = {
        "norm_g": np.ascontiguousarray(np.asarray(inp["norm_g"], f32).reshape(L, 16, 128).transpose(0, 2, 1)),
        "b_ada": np.ascontiguousarray(np.asarray(inp["b_ada"], f32).reshape(L, 48, 128).transpose(0, 2, 1)),
        "w_ada": np.ascontiguousarray(np.asarray(inp["w_ada"], f32)),
        "w_in": np.ascontiguousarray(np.asarray(inp["w_in"], f32)),
        "ln_g": np.ascontiguousarray(np.broadcast_to(np.asarray(inp["gmlp_ln_g"], f32)[:, None, :], (L, 128, W_A))),
        "ln_b": np.ascontiguousarray(np.broadcast_to(np.asarray(inp["gmlp_ln_b"], f32)[:, None, :], (L, 128, W_A))),
        "wsT": np.ascontiguousarray(np.asarray(inp["gmlp_ws"], f32).transpose(0, 3, 1, 2)),
        "bs": np.ascontiguousarray(np.broadcast_to(np.asarray(inp["gmlp_bs"], f32)[:, None, :, :], (L, 128, 8, 128))),
        "qkg": np.ascontiguousarray(np.stack([np.asarray(inp["q_norm_g"], f32), np.asarray(inp["k_norm_g"], f32)], axis=-1)
                                    .transpose(1, 0, 2)),
        "rpbt": rpbt,
        "w_pa": np.ascontiguousarray(np.asarray(inp["w_pa"], f32)),
        "w_pf": np.ascontiguousarray(np.asarray(inp["w_pf"], f32)),
        "w_pn": np.ascontiguousarray(np.asarray(inp["w_pn"], f32)),
        "w_out": np.ascontiguousarray(np.asarray(inp["w_out"], f32)),
        "ccsc": consts["ccsc"], "cnL": consts["cnL"], "snL": consts["snL"], "cnC": consts["cnC"], "snC": consts["snC"],
        "ident": consts["ident"],
    }
    x = np.asarray(inp["x"], f32)
    ctx = np.asarray(inp["ctx"], f32)
    c = np.asarray(inp["c"], f32)
    c_ctx = np.asarray(inp["c_ctx"], f32)
    maps = []
    for b in range(NCORES):
        c2 = np.stack([c[b], c_ctx], axis=-1).reshape(16, 128, 2).transpose(1, 0, 2)
        m = dict(shared)
        m["x"] = np.ascontiguousarray(x[b])
        m["ctx"] = np.ascontiguousarray(ctx[b])
        m["c2"] = np.ascontiguousarray(c2)
        maps.append(m)
    return maps


def kernel(**inputs):
    if "nc" not in _CACHE:
        _CACHE["nc"] = build_program()[0]
    nc = _CACHE["nc"]
    maps = _prep_inputs(inputs)
    res = run_bass_kernel_spmd(nc, maps, core_ids=list(range(NCORES)))
    out = np.stack([np.asarray(r["out"], np.float32) for r in res.results], axis=0)
    return out
```
